# Optimizing a Trainium2 kernel written in Bass

```python
import math
import jax, jax.numpy as jnp
from jax import lax
import numpy as np

D_MODEL = 1024
BATCH = 4
SEQ = 8192
DEPTH = 2

D_SSD = D_MODEL
SSD_HEADDIM = 64
SSD_HEADS = D_SSD // SSD_HEADDIM
SSD_GROUPS = 4
SSD_HPG = SSD_HEADS // SSD_GROUPS
D_STATE = 128
SSD_CONV = 4
CHUNK = 128
D_FOX = D_MODEL
FOX_HEADDIM = 64
FOX_HEADS = D_FOX // FOX_HEADDIM
Q_BLOCK = 128
D_MIX = D_SSD + D_FOX
EVEN_SIZES = (D_MIX, D_SSD + 2 * SSD_GROUPS * D_STATE, SSD_HEADS, D_FOX, D_FOX, D_FOX, FOX_HEADS)
EVEN_IN = D_MIX + D_SSD + 2 * SSD_GROUPS * D_STATE + SSD_HEADS + 3 * D_FOX + FOX_HEADS
D_CONV = 2 * D_MODEL
CONV_WIDTH = 31
ODD_IN = 3 * D_CONV
N_EVEN = (DEPTH + 1) // 2
N_ODD = DEPTH // 2
EPS = 1e-6

kernel_name = "hybrid_ssd_fox_conformer_trunk"


def _split_points(sizes):
    pts, acc = [], 0
    for s in sizes[:-1]:
        acc += s
        pts.append(acc)
    return pts


def rmsnorm(x, g):
    xf = x.astype(jnp.float32)
    y = xf * lax.rsqrt(jnp.mean(xf * xf, axis=-1, keepdims=True) + EPS)
    return (y * g.astype(jnp.float32)).astype(x.dtype)


def layernorm(x, g, b):
    xf = x.astype(jnp.float32)
    mu = jnp.mean(xf, axis=-1, keepdims=True)
    xc = xf - mu
    y = xc * lax.rsqrt(jnp.mean(xc * xc, axis=-1, keepdims=True) + EPS)
    return (y * g.astype(jnp.float32) + b.astype(jnp.float32)).astype(x.dtype)


def causal_depthwise_conv(x, w, b):
    k, c = w.shape
    y = lax.conv_general_dilated(x, w[:, None, :], window_strides=(1,), padding=[(k - 1, 0)],
                                 dimension_numbers=('NWC', 'WIO', 'NWC'), feature_group_count=c)
    return y + b


def ssd_chunked(x, dt, a, bm, cm):
    bsz, s, g, r, p = x.shape
    n = bm.shape[-1]
    nc = s // CHUNK
    xd = (x * dt[..., None]).reshape(bsz, nc, CHUNK, g, r, p)
    da = (dt * a).reshape(bsz, nc, CHUNK, g, r)
    bc = bm.reshape(bsz, nc, CHUNK, g, n)
    cc = cm.reshape(bsz, nc, CHUNK, g, n)
    cs = jnp.cumsum(da, axis=2)
    li = jnp.arange(CHUNK)
    causal = (li[:, None] >= li[None, :])[None, None, :, :, None, None]
    seg = cs[:, :, :, None] - cs[:, :, None, :]
    decay = jnp.exp(jnp.where(causal, seg, -jnp.inf))
    cb = jnp.einsum('bclgn,bcsgn->bclsg', cc, bc)
    y_diag = jnp.einsum('bclsgr,bcsgrp->bclgrp', cb[..., None] * decay, xd)
    decay_to_end = jnp.exp(cs[:, :, -1:] - cs)
    chunk_states = jnp.einsum('bclgn,bclgrp->bcgrpn', bc, xd * decay_to_end[..., None])
    chunk_decay = jnp.exp(cs[:, :, -1])

    def step(h, inp):
        st, dec = inp
        h_new = h * dec[..., None, None] + st
        return h_new, h

    h0 = jnp.zeros((bsz, g, r, p, n), chunk_states.dtype)
    _, h_in = lax.scan(step, h0, (jnp.moveaxis(chunk_states, 1, 0), jnp.moveaxis(chunk_decay, 1, 0)))
    h_in = jnp.moveaxis(h_in, 0, 1)
    y_off = jnp.einsum('bclgn,bcgrpn->bclgrp', cc, h_in) * jnp.exp(cs)[..., None]
    return (y_diag + y_off).reshape(bsz, s, g, r, p)


def forgetting_attention(q, k, v, log_f):
    bsz, s, h, p = q.shape
    c = jnp.cumsum(log_f, axis=1)
    ck = jnp.transpose(c, (0, 2, 1))
    nb = s // Q_BLOCK
    qb = jnp.moveaxis(q.reshape(bsz, nb, Q_BLOCK, h, p), 1, 0)
    cqb = jnp.moveaxis(c.reshape(bsz, nb, Q_BLOCK, h), 1, 0)
    kpos = jnp.arange(s)
    scale = p ** -0.5

    def block(args):
        qi, cqi, i = args
        qpos = i * Q_BLOCK + jnp.arange(Q_BLOCK)
        logits = jnp.einsum('bqhp,bkhp->bhqk', qi, k).astype(jnp.float32) * scale
        logits = logits + jnp.transpose(cqi, (0, 2, 1))[..., None] - ck[:, :, None, :]
        logits = jnp.where(kpos[None, :] <= qpos[:, None], logits, -jnp.inf)
        w = jax.nn.softmax(logits, axis=-1).astype(v.dtype)
        return jnp.einsum('bhqk,bkhp->bqhp', w, v)

    out = lax.map(block, (qb, cqb, jnp.arange(nb)))
    return jnp.moveaxis(out, 0, 1).reshape(bsz, s, h, p)


def ssd_fox_layer(x, g_pre, w_in, conv_w, conv_b, dt_bias, a_log, d_skip, fgate_b, ssd_norm, w_out, g_post):
    bsz, s, _ = x.shape
    u = rmsnorm(x, g_pre)
    proj = jnp.einsum('bsd,de->bse', u, w_in)
    z, xbc, dt_raw, q, k, v, f_raw = jnp.split(proj, _split_points(EVEN_SIZES), axis=-1)
    z_ssd, z_fox = jnp.split(z, [D_SSD], axis=-1)
    xbc = jax.nn.silu(causal_depthwise_conv(xbc, conv_w, conv_b))
    xs, bm, cm = jnp.split(xbc, [D_SSD, D_SSD + SSD_GROUPS * D_STATE], axis=-1)
    dt = jax.nn.softplus(dt_raw + dt_bias).reshape(bsz, s, SSD_GROUPS, SSD_HPG)
    a = (-jnp.exp(a_log)).reshape(SSD_GROUPS, SSD_HPG)
    xs = xs.reshape(bsz, s, SSD_GROUPS, SSD_HPG, SSD_HEADDIM)
    y = ssd_chunked(xs, dt, a, bm.reshape(bsz, s, SSD_GROUPS, D_STATE), cm.reshape(bsz, s, SSD_GROUPS, D_STATE))
    y = (y + xs * d_skip.reshape(SSD_GROUPS, SSD_HPG)[:, :, None]).reshape(bsz, s, D_SSD)
    yg = (y * jax.nn.silu(z_ssd)).reshape(bsz, s, SSD_GROUPS, D_SSD // SSD_GROUPS).astype(jnp.float32)
    yg = yg * lax.rsqrt(jnp.mean(yg * yg, axis=-1, keepdims=True) + EPS)
    y = (yg.reshape(bsz, s, D_SSD) * ssd_norm.astype(jnp.float32)).astype(x.dtype)
    log_f = jax.nn.log_sigmoid((f_raw + fgate_b).astype(jnp.float32))
    o = forgetting_attention(q.reshape(bsz, s, FOX_HEADS, FOX_HEADDIM),
                             k.reshape(bsz, s, FOX_HEADS, FOX_HEADDIM),
                             v.reshape(bsz, s, FOX_HEADS, FOX_HEADDIM), log_f)
    o = o.reshape(bsz, s, D_FOX) * jax.nn.silu(z_fox)
    out = jnp.einsum('bse,ed->bsd', jnp.concatenate([y, o], axis=-1), w_out)
    return x + rmsnorm(out, g_post)


def conformer_conv_layer(x, g_pre, w_in, conv_w, conv_b, ln_g, ln_b, w_out, g_post):
    u = rmsnorm(x, g_pre)
    proj = jnp.einsum('bsd,de->bse', u, w_in)
    val, glu_gate, z = jnp.split(proj, [D_CONV, 2 * D_CONV], axis=-1)
    h = val * jax.nn.sigmoid(glu_gate)
    h = causal_depthwise_conv(h, conv_w, conv_b)
    h = jax.nn.silu(layernorm(h, ln_g, ln_b))
    h = h * jax.nn.silu(z)
    out = jnp.einsum('bse,ed->bsd', h, w_out)
    return x + rmsnorm(out, g_post)


def setup_inputs(seed: int = 0) -> dict:
    key = jax.random.key(seed)
    ks = jax.random.split(key, 20)
    f32 = jnp.float32

    def nrm(k, shape, scale):
        return jax.random.normal(k, shape, f32) * scale

    def gain(k, shape):
        return 1.0 + 0.02 * jax.random.normal(k, shape, f32)

    ne, no = N_EVEN, N_ODD
    x = jax.random.normal(ks[0], (BATCH, SEQ, D_MODEL), f32)
    dt0 = jnp.exp(jax.random.uniform(ks[5], (ne, SSD_HEADS), f32, minval=math.log(1e-3), maxval=math.log(1e-1)))
    e_dt_bias = dt0 + jnp.log(-jnp.expm1(-dt0))
    e_a_log = jnp.log(jax.random.uniform(ks[6], (ne, SSD_HEADS), f32, minval=1.0, maxval=16.0))
    return {
        "x": x,
        "e_norm_pre": gain(ks[1], (ne, D_MODEL)),
        "e_w_in": nrm(ks[2], (ne, D_MODEL, EVEN_IN), D_MODEL ** -0.5),
        "e_conv_w": nrm(ks[3], (ne, SSD_CONV, D_SSD + 2 * SSD_GROUPS * D_STATE), SSD_CONV ** -0.5),
        "e_conv_b": nrm(ks[4], (ne, D_SSD + 2 * SSD_GROUPS * D_STATE), 0.02),
        "e_dt_bias": e_dt_bias,
        "e_a_log": e_a_log,
        "e_d_skip": 1.0 + 0.1 * jax.random.normal(ks[7], (ne, SSD_HEADS), f32),
        "e_fgate_b": jax.random.uniform(ks[8], (ne, FOX_HEADS), f32, minval=1.0, maxval=6.0),
        "e_ssd_norm": gain(ks[9], (ne, D_SSD)),
        "e_w_out": nrm(ks[10], (ne, D_MIX, D_MODEL), D_MIX ** -0.5),
        "e_norm_post": gain(ks[11], (ne, D_MODEL)),
        "o_norm_pre": gain(ks[12], (no, D_MODEL)),
        "o_w_in": nrm(ks[13], (no, D_MODEL, ODD_IN), D_MODEL ** -0.5),
        "o_conv_w": nrm(ks[14], (no, CONV_WIDTH, D_CONV), CONV_WIDTH ** -0.5),
        "o_conv_b": nrm(ks[15], (no, D_CONV), 0.02),
        "o_ln_g": gain(ks[16], (no, D_CONV)),
        "o_ln_b": nrm(ks[17], (no, D_CONV), 0.02),
        "o_w_out": nrm(ks[18], (no, D_CONV, D_MODEL), D_CONV ** -0.5),
        "o_norm_post": gain(ks[19], (no, D_MODEL)),
    }


def reference(x, e_norm_pre, e_w_in, e_conv_w, e_conv_b, e_dt_bias, e_a_log, e_d_skip, e_fgate_b,
              e_ssd_norm, e_w_out, e_norm_post, o_norm_pre, o_w_in, o_conv_w, o_conv_b, o_ln_g, o_ln_b,
              o_w_out, o_norm_post):
    for layer in range(DEPTH):
        i = layer // 2
        if layer % 2 == 0:
            x = ssd_fox_layer(x, e_norm_pre[i], e_w_in[i], e_conv_w[i], e_conv_b[i], e_dt_bias[i],
                              e_a_log[i], e_d_skip[i], e_fgate_b[i], e_ssd_norm[i], e_w_out[i],
                              e_norm_post[i])
        else:
            x = conformer_conv_layer(x, o_norm_pre[i], o_w_in[i], o_conv_w[i], o_conv_b[i], o_ln_g[i],
                                     o_ln_b[i], o_w_out[i], o_norm_post[i])
    return x
```

```python
import numpy as np
from contextlib import ExitStack
import ml_dtypes
import concourse.bass as bass
import concourse.mybir as mybir
from concourse.bass_utils import run_bass_kernel_spmd

AF = mybir.ActivationFunctionType
ALU = mybir.AluOpType
F32 = mybir.dt.float32
BF16 = mybir.dt.bfloat16
EPS = 1e-6


class Prog:
    NDMASEM = 12

    def __init__(self, nc):
        self.nc = nc
        self.ops = []
        self.last_w = {}
        self.readers = {}
        self.bar = None
        self.bar_done = set()
        self.last_on = {}
        self.dma_recent = {}

    def barrier(self):
        b = list(self.last_on.values())
        for q in self.dma_recent.values():
            b.extend(q)
        self.bar = b
        self.bar_done = set()

    BANK = {'pb0': ('B0',), ('pj', 0): ('B0',), ('pj', 1): ('B1',), ('pj', 2): ('B2',), ('pj', 3): ('B3',),
            'pd': ('B2',), 'pc': ('B2',), 'pt8': ('B2',), 'pr': ('B3',),
            'pcb': ('B4',), 'pst': ('B4',), 'py': ('B5',), 'pmean': ('B5',),
            'pn': ('B6',), 'pmsq': ('B6',), 'ptb': ('B7',), ('pcv', 0): ('B3',), ('pcv', 1): ('B7',),
            ('st', 0): ('B0', 'B1'), ('st', 1): ('B2', 'B3'), ('st', 2): ('B4', 'B5'), ('ob', 0): ('B6',), ('ob', 1): ('B7',)}

    def op(self, eng, fn, r=(), w=(), dma=False, cc=False):
        idx = len(self.ops)
        deps = {}
        banks = set()
        for k in r:
            if k in self.BANK:
                banks.update(self.BANK[k])
        for k in w:
            if k in self.BANK:
                banks.update(self.BANK[k])
        if banks:
            w = list(w) + sorted(banks)
        for k in r:
            lw = self.last_w.get(k)
            if lw is not None:
                deps[lw] = True
        for k in w:
            lw = self.last_w.get(k)
            if lw is not None:
                deps[lw] = True
            rd = self.readers.get(k)
            if rd:
                for i in rd[0].values():
                    if i not in deps:
                        deps[i] = False
                for i in rd[1]:
                    if i not in deps:
                        deps[i] = False
        if self.bar is not None and eng not in self.bar_done:
            for i in self.bar:
                deps[i] = True
            self.bar_done.add(eng)
        for k in r:
            rd = self.readers.setdefault(k, ({}, []))
            if dma:
                rd[1].append(idx)
            else:
                rd[0][eng] = idx
        for k in w:
            self.last_w[k] = idx
            self.readers[k] = ({}, [])
        self.ops.append(dict(eng=eng, fn=fn, deps=deps, dma=dma, cc=cc))
        if dma and not cc:
            q = self.dma_recent.setdefault(eng, [])
            q.append(idx)
            if len(q) > self.NDMASEM:
                q.pop(0)
        else:
            self.last_on[eng] = idx
        return idx

    def _skip(self, od, o):
        return (not od['dma']) and (not o['dma']) and od['eng'] == o['eng']

    def finalize(self, sems):
        ops = self.ops
        n = len(ops)
        need = [False] * n
        for o in ops:
            for i, hard in o['deps'].items():
                oi = ops[i]
                if oi['dma']:
                    continue
                if self._skip(oi, o) and (oi['eng'] == 'pe' or not hard):
                    continue
                need[i] = True
        cnt = {}
        dcnt = {}
        sig = [None] * n
        P = self.NDMASEM
        for i, o in enumerate(ops):
            e = o['eng']
            if o['cc']:
                k = dcnt.get('cc', 0)
                dcnt['cc'] = k + 1
                sig[i] = (sems['cc'][k], 1)
            elif o['dma']:
                k = dcnt.get(e, 0)
                dcnt[e] = k + 1
                sig[i] = (sems['dma_' + e][k % P], 16 * (k // P + 1))
            elif need[i]:
                cnt[e] = cnt.get(e, 0) + 1
                sig[i] = (sems[e], cnt[e])
        self.sig = sig
        engs = {}
        for i, o in enumerate(ops):
            engs.setdefault(o['eng'], []).append(i)
        self.engs = engs

    def final_values(self):
        last = {}
        for i, o in enumerate(self.ops):
            if self.sig[i] is not None:
                s = self.sig[i]
                prev = last.get(id(s[0]))
                if prev is None or prev[1] < s[1]:
                    last[id(s[0])] = (s[0], s[1])
        return list(last.values())

    def emit_engine(self, e, h, final=False):
        ops = self.ops
        sig = self.sig
        waited = {}
        for sem, val in self.pre_waits:
            h.wait_ge(sem, val)

        def wait(sem, val):
            key = id(sem)
            if waited.get(key, 0) >= val:
                return
            waited[key] = val
            h.wait_ge(sem, val)

        for i in self.engs.get(e, []):
            o = ops[i]
            for d, hard in o['deps'].items():
                od = ops[d]
                if self._skip(od, o) and (e == 'pe' or not hard):
                    continue
                s = sig[d]
                wait(s[0], s[1])
            if o['cc']:
                o['fn'](h).then_inc(sig[i][0])
            elif o['dma']:
                sem, val = sig[i]
                if val > 16:
                    wait(sem, val - 16)
                o['fn'](h).then_inc(sem, 16)
            else:
                ins = o['fn'](h)
                if sig[i] is not None:
                    ins.then_inc(sig[i][0], 1)
        if final:
            last = {}
            for i, o in enumerate(ops):
                if sig[i] is not None:
                    s = sig[i]
                    prev = last.get(id(s[0]))
                    if prev is None or prev[1] < s[1]:
                        last[id(s[0])] = (s[0], s[1])
            for sem, val in last.values():
                wait(sem, val)

    @classmethod
    def make_sems(cls, nc, es, tag=""):
        sems = {}
        for e in ['pe', 'act', 'dve', 'pool']:
            sems[e] = es.enter_context(nc.semaphore(tag + "s_" + e))
        sems['dma_sp'] = [es.enter_context(nc.semaphore(tag + "dq%d" % i)) for i in range(cls.NDMASEM)]
        sems['dma_pool'] = [es.enter_context(nc.semaphore(tag + "dp%d" % i)) for i in range(cls.NDMASEM)]
        sems['cc'] = [es.enter_context(nc.semaphore(tag + "ccs%d" % i)) for i in range(8)]
        return sems

    def run(self, es, sems=None, pre_waits=()):
        nc = self.nc
        if sems is None:
            sems = self.make_sems(nc, es)
        self.pre_waits = list(pre_waits)
        self.finalize(sems)
        p = self
        with nc.Block() as block:
            @block.tensor
            def _(e):
                p.emit_engine('pe', e)

            @block.scalar
            def _(e):
                p.emit_engine('act', e)

            @block.vector
            def _(e):
                p.emit_engine('dve', e)

            @block.gpsimd
            def _(e):
                p.emit_engine('pool', e)

            @block.sync
            def _(e):
                p.emit_engine('sp', e, final=True)


class Arena:
    def __init__(self, t, nwords):
        self.t = t
        self.n = nwords
        self.off = 0

    def reset(self):
        self.off = 0

    def _shape(self, v, shape):
        if len(shape) == 2:
            return v
        if len(shape) == 3:
            return v.rearrange("p (a b) -> p a b", a=shape[1])
        raise ValueError(shape)

    def f32(self, shape):
        n = int(np.prod(shape[1:]))
        v = self.t[0:shape[0], self.off:self.off + n]
        self.off += n
        assert self.off <= self.n, (self.off, self.n)
        return self._shape(v, shape)

    def bf16(self, shape):
        n = int(np.prod(shape[1:]))
        w = (n + 1) // 2
        v = self.t[0:shape[0], self.off:self.off + w].bitcast(BF16)
        self.off += w
        assert self.off <= self.n, (self.off, self.n)
        if 2 * w != n:
            v = v[:, 0:n]
        return self._shape(v, shape)


def _consts_np():
    U = np.triu(np.ones((128, 128), np.float32))
    ident = np.eye(128, dtype=np.float32)
    swap = np.zeros((128, 128), np.float32)
    for k in range(128):
        swap[k, (k + 64) % 128] = 1.0
    mj = np.zeros((128, 4, 512), np.float32)
    kk = np.arange(128)[:, None]
    qq = np.arange(512)[None, :]
    for j in range(4):
        mj[:, j, :] = np.where(qq < 128 * j + kk, -30000.0, 0.0)
    cf = np.concatenate([U, ident, swap], axis=1)
    cb = np.concatenate([ident, mj.reshape(128, 2048)], axis=1).astype(ml_dtypes.bfloat16)
    return cf, cb


def build_A(nc, T, stages=('s0', 'ssd', 'f0', 'fox'), fused=None):
    NB = T // 512
    NCH = T // 128
    D = lambda name, shape, dt, kind=None: (nc.dram_tensor(name, shape, dt, kind=kind).ap() if kind else nc.dram_tensor(name, shape, dt).ap())
    xT = D("xT", [1024, T], F32, "ExternalInput")
    gpre = D("gpre", [128, 8], F32, "ExternalInput")
    wssd = D("wssd", [2, 1024, 768], F32, "ExternalInput")
    wdt = D("wdt", [1024, 8], F32, "ExternalInput")
    cw = D("cw", [128, 8, 4], F32, "ExternalInput")
    cbv = D("cbv", [128, 8], F32, "ExternalInput")
    dtb16 = D("dtb16", [128, 2, 16], F32, "ExternalInput")
    alog16 = D("alog16", [128, 2, 16], F32, "ExternalInput")
    dsk = D("dsk", [128, 4], F32, "ExternalInput")
    ssdn = D("ssdn", [128, 4], F32, "ExternalInput")
    wfox = D("wfox", [8, 1024, 192], F32, "ExternalInput")
    wzf = D("wzf", [1024, 512], F32, "ExternalInput")
    wf = D("wf", [1024, 8], F32, "ExternalInput")
    fb = D("fb", [8, 1], F32, "ExternalInput")
    cfd = D("cfd", [128, 384], F32, "ExternalInput")
    cbd = D("cbd", [128, 2176], BF16, "ExternalInput")
    cneg = D("cneg", [3, 512], BF16, "ExternalInput")
    mixT = None if fused else D("mixT", [1024, T], BF16, "ExternalOutput")

    def mix_dst(r0, n, blk):
        if fused:
            return fused["mixloc_bf"][r0 // 128][r0 % 128:r0 % 128 + n, blk]
        return mixT[r0:r0 + n, blk]
    crow = D("crow", [3, 8, T], BF16)

    with ExitStack() as es:
        sb = lambda n, s, d: es.enter_context(nc.sbuf_tensor(n, s, d))
        ps = lambda n, s, d: es.enter_context(nc.psum_tensor(n, s, d))
        uT = sb("uT", [128, 8, T], BF16)
        cf = sb("cf", [128, 384], F32)
        cb = sb("cb", [128, 2176], BF16)
        U = cf[:, 0:128]
        identf = cf[:, 128:256]
        swapm = cf[:, 256:384]
        identb = cb[:, 0:128]
        maskJ = cb[:, 128:2176].rearrange("p (j q) -> p j q", j=4)
        sm = sb("sm", [128, 160], F32)
        gpre_s = sm[:, 0:8]
        cbv_s = sm[:, 8:16]
        dsk_s = sm[:, 16:20]
        ssdn_s = sm[:, 20:24]
        epsb = sm[:, 24:25]
        oneb = sm[:, 25:26]
        nfb = sm[:, 26:27]
        fb_s = sm[:, 27:28]
        dtb_s = sm[:, 32:64].rearrange("p (g c) -> p g c", g=2)
        a16 = sm[:, 64:96].rearrange("p (g c) -> p g c", g=2)
        cw_s = sm[:, 96:128].rearrange("p (t k) -> p t k", t=8)
        onesb = sb("onesb", [128, 256], BF16)
        onesf = sb("onesf", [128, 128], F32)
        ncol = sb("ncol", [128, NCH, 8], F32)
        AW = 17408
        arena_t = sb("arena", [128, AW], F32)
        ar = Arena(arena_t, AW)
        PP = [ps("pp%d" % i, [128, 1024], F32) for i in range(4)]
        PB = [PP[i // 2][:, (i % 2) * 512:(i % 2) * 512 + 512] for i in range(8)]
        PTB = PB[7].bitcast(BF16)

        p = Prog(nc)
        mo_keys = [[] for _ in range(8)]
        dma = lambda out, in_, r, w: p.op('sp', lambda e: e.dma_start(out=out, in_=in_), r=r, w=w, dma=True)

        if fused:
            cast_q = list(fused['cast_pieces'])
        else:
            cast_q = []

        def emit_casts(n):
            for _ in range(n):
                if cast_q:
                    src, dst = cast_q.pop(0)
                    p.op('pool', lambda e, src=src, dst=dst: e.dma_start(out=dst, in_=src), r=[], w=[], dma=True)
        if True:
            pass
        dma(cf[:], cfd, [], ['cf'])
        dma(cb[:], cbd, [], ['cb'])
        dma(gpre_s, gpre, [], ['sm'])
        dma(cbv_s, cbv, [], ['sm'])
        dma(dsk_s, dsk, [], ['sm'])
        dma(ssdn_s, ssdn, [], ['sm'])
        dma(fb_s[0:8, :], fb, [], ['sm'])
        dma(dtb_s, dtb16, [], ['sm'])
        dma(a16, alog16, [], ['sm'])
        dma(cw_s, cw, [], ['sm'])
        p.op('pool', lambda e: e.memset(epsb, EPS), w=['c1'])
        p.op('pool', lambda e: e.memset(oneb, 1.0), w=['c1'])
        p.op('pool', lambda e: e.memset(onesb[:, 0:128], 1.0 / 1024), w=['c1'])
        p.op('pool', lambda e: e.memset(onesb[:, 128:256], 1.0 / 256), w=['c1'])
        p.op('pool', lambda e: e.memset(onesf[:], 1.0), w=['c1'])
        p.op('dve', lambda e: e.tensor_scalar(out=nfb[0:8, :], in0=fb_s[0:8, :], scalar1=-1.0, scalar2=None, op0=ALU.mult), r=['sm'], w=['nfb'])
        p.op('act', lambda e: e.activation(out=a16, in_=a16, func=AF.Exp), r=['sm'], w=['a16'])
        p.op('dve', lambda e: e.tensor_scalar(out=a16, in0=a16, scalar1=-1.0, scalar2=None, op0=ALU.mult), r=['a16'], w=['a16'])
        CK = ['cf', 'cb', 'sm', 'c1', 'nfb', 'a16']

        if 's0' in stages:
            ar.reset()
            XB = [ar.f32([128, 8, 512]) for _ in range(2)]
            SQ = [ar.bf16([128, 8, 512]) for _ in range(2)]
            TMP = [ar.f32([128, 512]) for _ in range(2)]
            RS = [ar.f32([128, 512]) for _ in range(2)]
            for tb in range(NB):
                i = tb % 2
                xb, sq, tmp, rs = XB[i], SQ[i], TMP[i], RS[i]
                blk = slice(tb * 512, (tb + 1) * 512)
                dma(xb, xT[:, blk].rearrange("(k p) t -> p k t", p=128), [], [('xb', i)])
                p.op('pool', lambda e, xb=xb, sq=sq: e.tensor_tensor(out=sq, in0=xb, in1=xb, op=ALU.mult), r=[('xb', i)], w=[('sq', i)])
                for k in range(8):
                    p.op('pe', lambda e, k=k, sq=sq: e.matmul(PB[0][:, :], lhsT=onesb[:, 0:128], rhs=sq[:, k, :], start=(k == 0), stop=(k == 7)),
                         r=[('sq', i), 'c1'], w=['pb0'])
                p.op('act', lambda e, tmp=tmp: e.activation(out=tmp, in_=PB[0][:, :], func=AF.Sqrt, bias=epsb), r=['pb0', 'c1'], w=[('tmp', i)])
                p.op('dve', lambda e, tmp=tmp, rs=rs: e.reciprocal(out=rs, in_=tmp), r=[('tmp', i)], w=[('rs', i)])
                for k in range(8):
                    p.op('dve', lambda e, k=k, xb=xb, rs=rs, blk=blk: e.scalar_tensor_tensor(
                        out=uT[:, k, blk], in0=xb[:, k, :], scalar=gpre_s[:, k:k + 1], in1=rs, op0=ALU.mult, op1=ALU.mult),
                        r=[('xb', i), ('rs', i), 'sm'], w=[('uT', tb)])
            p.barrier()

        if 'ssd' in stages:
            ar.reset()
            wst = [ar.f32([128, 8, 128])] * 2
            wg = ar.bf16([128, 8, 768])
            wdt_f = ar.f32([128, 8, 8])
            wdt_b = ar.bf16([128, 8, 8])
            XC = ar.f32([128, 4, 515])
            zs = ar.bf16([128, 2, 512])
            ACC = [ar.f32([128, 512]) for _ in range(2)]
            ft = ar.bf16([128, 4, 512])
            xs_tok2 = [ar.bf16([128, 256]) for _ in range(2)]
            B_tok2 = [ar.bf16([128, 128]) for _ in range(2)]
            DTB = [dict(dtr=ar.f32([128, 16]), dte1=ar.f32([128, 16]), dtv=ar.f32([128, 16]), dav=ar.f32([128, 16]),
                        csv=ar.f32([128, 16]), ncs=ar.f32([128, 16])) for _ in range(2)]
            clv2 = [ar.f32([128, 4]) for _ in range(2)]
            dtmp = ar.f32([128, 4])
            dtev = ar.f32([128, 4])
            cdv = ar.f32([128, 4])
            UD2 = [ar.f32([128, 4, 128]) for _ in range(2)]
            Ev2 = [ar.f32([128, 4, 128]) for _ in range(2)]
            E2v2 = [ar.f32([128, 4, 128]) for _ in range(2)]
            MT2 = [ar.bf16([128, 4, 128]) for _ in range(2)]
            CsT2 = [ar.bf16([128, 4, 128]) for _ in range(2)]
            CBm2 = [ar.f32([128, 128]) for _ in range(2)]
            xdp2 = [ar.bf16([128, 4, 128]) for _ in range(2)]
            xdd = ar.bf16([128, 256])
            hT = ar.f32([128, 256])
            hTp = ar.bf16([128, 4, 128])
            Y = ar.f32([128, 2, 512])
            sqy = ar.bf16([128, 2, 512])
            tmpn = ACC[0]
            rsn = ACC[1]
            ob = ar.bf16([128, 2, 512])
            PJ = [PB[0], PB[1]]
            PM = PB[2]
            PR = PB[3]
            PCB = PB[4]
            PY = PB[5]
            PN = PB[6]
            dma(wdt_f, wdt.rearrange("(k p) c -> p k c", p=128), [], ['wdt_f'])
            p.op('pool', lambda e: e.tensor_copy(out=wdt_b, in_=wdt_f), r=['wdt_f'], w=['wdt_b'])
            pj_i = 0
            ch_i = 0
            dt_i = 0

            def dt_stage(g, tb, di):
                d = DTB[di]
                dtr, dte1, dtv, dav, csv, ncs = d['dtr'], d['dte1'], d['dtv'], d['dav'], d['csv'], d['ncs']
                K = lambda n: (n, di)
                for c in range(4):
                    tok = slice(tb * 512 + c * 128, tb * 512 + (c + 1) * 128)
                    for k in range(8):
                        p.op('pe', lambda e, c=c, k=k, tok=tok: e.matmul(PM[:, c * 4:(c + 1) * 4], lhsT=uT[:, k, tok], rhs=wdt_b[:, k, g * 4:(g + 1) * 4],
                                                                       start=(k == 0), stop=(k == 7)), r=[('uT', tb), 'wdt_b'], w=['pd'])
                p.op('dve', lambda e: e.tensor_tensor(out=dtr, in0=PM[:, 0:16], in1=dtb_s[:, g, :], op=ALU.add), r=['pd', 'sm'], w=[K('dtr')])
                p.op('act', lambda e: e.activation(out=dte1, in_=dtr, func=AF.Exp), r=[K('dtr')], w=[K('dte1')])
                p.op('act', lambda e: e.activation(out=dtv, in_=dte1, func=AF.Ln, bias=oneb), r=[K('dte1'), 'c1'], w=[K('dtv')])
                p.op('dve', lambda e: e.tensor_tensor(out=dav, in0=dtv, in1=a16[:, g, :], op=ALU.mult), r=[K('dtv'), 'a16'], w=[K('dav')])
                p.op('pe', lambda e: e.matmul(PM[:, 16:32], lhsT=U, rhs=dav, start=True, stop=True), r=['cf', K('dav')], w=['pc'])
                p.op('dve', lambda e: e.tensor_copy(out=csv, in_=PM[:, 16:32]), r=['pc'], w=[K('csv')])
                p.op('dve', lambda e: e.tensor_scalar(out=ncs, in0=csv, scalar1=-1.0, scalar2=None, op0=ALU.mult), r=[K('csv')], w=[K('ncs')])
            for g in range(2):
                for j in range(6):
                    st = wst[j % 2]
                    dma(st, wssd[g, :, j * 128:(j + 1) * 128].rearrange("(k p) c -> p k c", p=128), [], [('wst', 0)])
                    p.op('pool', lambda e, st=st, j=j: e.tensor_copy(out=wg[:, :, j * 128:(j + 1) * 128], in_=st), r=[('wst', 0)], w=['wg'])
                p.op('pool', lambda e: e.memset(XC[:, :, 0:3], 0.0), w=[('XC', j) for j in range(4)])
                for i_ in range(2):
                    p.op('pool', lambda e, i_=i_: e.memset(xdp2[i_], 0.0), w=[('xdp', i_)])
                p.op('pool', lambda e: e.memset(hTp, 0.0), w=['hTp'])
                p.op('pool', lambda e: e.memset(hT, 0.0), w=['hT'])
                dt_stage(g, 0, dt_i % 2)
                for tb in range(NB):
                    blk = slice(tb * 512, (tb + 1) * 512)
                    emit_casts(3 if NB >= 16 else 56)
                    for j in range(6):
                        pj = PJ[pj_i % 2]
                        pk = ('pj', pj_i % 2)
                        pj_i += 1
                        for k in range(8):
                            p.op('pe', lambda e, pj=pj, j=j, k=k, blk=blk: e.matmul(pj[:, :], lhsT=wg[:, k, j * 128:(j + 1) * 128], rhs=uT[:, k, blk],
                                                                                 start=(k == 0), stop=(k == 7)), r=['wg', ('uT', tb)], w=[pk])
                        if j < 2:
                            p.op('act', lambda e, pj=pj, j=j: e.activation(out=zs[:, j, :], in_=pj[:, :], func=AF.Silu), r=[pk], w=[('zs', j)])
                        else:
                            p.op('act', lambda e, pj=pj, j=j: e.activation(out=XC[:, j - 2, 3:515], in_=pj[:, :], func=AF.Copy), r=[pk], w=[('XC', j - 2)])
                    for j in range(4):
                        eng = 'dve'
                        acc = ACC[j % 2]
                        t = g * 4 + j
                        p.op(eng, lambda e, acc=acc, j=j, t=t: e.tensor_scalar(out=acc, in0=XC[:, j, 3:515], scalar1=cw_s[:, t, 3:4], scalar2=cbv_s[:, t:t + 1],
                                                                          op0=ALU.mult, op1=ALU.add), r=[('XC', j), 'sm'], w=[('acc', j % 2)])
                        for s in range(3):
                            p.op(eng, lambda e, acc=acc, j=j, t=t, s=s: e.scalar_tensor_tensor(out=acc, in0=XC[:, j, s:s + 512], scalar=cw_s[:, t, s:s + 1], in1=acc,
                                                                                           op0=ALU.mult, op1=ALU.add), r=[('XC', j), ('acc', j % 2)], w=[('acc', j % 2)])
                        p.op('act', lambda e, acc=acc, j=j: e.activation(out=ft[:, j, :], in_=acc, func=AF.Silu), r=[('acc', j % 2)], w=[('ft', j)])
                        p.op(eng, lambda e, j=j: e.tensor_copy(out=XC[:, j, 0:3], in_=XC[:, j, 512:515]), r=[('XC', j)], w=[('XC', j)])
                    di = dt_i % 2
                    dt_i += 1
                    if tb + 1 < NB:
                        dt_stage(g, tb + 1, dt_i % 2)
                    dtv, dav, csv, ncs = DTB[di]['dtv'], DTB[di]['dav'], DTB[di]['csv'], DTB[di]['ncs']
                    KD = lambda n, di=di: (n, di)
                    def front(c, ci, tb=tb, g=g, dtv=dtv, dav=dav, ncs=ncs, KD=KD):
                        ck = slice(c * 128, (c + 1) * 128)
                        xs_tok, B_tok, UD, Ev, E2v, MT, CsT, CBm, xdp, clv = xs_tok2[ci], B_tok2[ci], UD2[ci], Ev2[ci], E2v2[ci], MT2[ci], CsT2[ci], CBm2[ci], xdp2[ci], clv2[ci]
                        K = lambda n: (n, ci)
                        for j in range(3):
                            src = ft[:, j if j < 2 else 2, ck]
                            p.op('pe', lambda e, j=j, src=src: e.transpose(out=PTB[:, j * 128:(j + 1) * 128], in_=src, identity=identb),
                                 r=[('ft', j if j < 2 else 2), 'cb'], w=['ptb'])
                        p.op('dve', lambda e: e.tensor_copy(out=xs_tok, in_=PTB[:, 0:256]), r=['ptb'], w=[K('xs_tok')])
                        p.op('dve', lambda e: e.tensor_copy(out=B_tok, in_=PTB[:, 256:384]), r=['ptb'], w=[K('B_tok')])
                        for h in range(4):
                            hs = slice((h % 2) * 64, (h % 2) * 64 + 64)
                            p.op('dve', lambda e, h=h, hs=hs: e.tensor_scalar(out=xdp[:, h, hs], in0=xs_tok[:, h * 64:(h + 1) * 64],
                                                                          scalar1=dtv[:, c * 4 + h:c * 4 + h + 1], scalar2=None, op0=ALU.mult),
                                 r=[K('xs_tok'), KD('dtv')], w=[K('xdp')])
                        for h in range(4):
                            p.op('pe', lambda e, h=h: e.matmul(PR[:, h * 128:(h + 1) * 128], lhsT=dav[:, c * 4 + h:c * 4 + h + 1].to_broadcast([128, 128]), rhs=U, start=True, stop=True),
                                 r=['cf', KD('dav')], w=['pr'])
                        p.op('pe', lambda e: e.matmul(PCB[:, 0:128], lhsT=ft[:, 2, ck], rhs=ft[:, 3, ck], start=True, stop=True),
                             r=[('ft', 2), ('ft', 3)], w=['pcb'])
                        p.op('dve', lambda e: e.tensor_tensor(out=CBm, in0=PCB[:, 0:128], in1=U, op=ALU.mult), r=['pcb', 'cf'], w=[K('CBm')])
                        for h in range(4):
                            p.op('dve', lambda e, h=h: e.tensor_scalar(out=Ev[:, h, :], in0=PR[:, h * 128:(h + 1) * 128], scalar1=ncs[:, c * 4 + h:c * 4 + h + 1],
                                                                    scalar2=0.0, op0=ALU.add, op1=ALU.min), r=['pr', KD('ncs')], w=[K('Ev')])
                        p.op('act', lambda e: e.activation(out=E2v, in_=PR[:, :].rearrange("p (h l) -> p h l", h=4), func=AF.Exp), r=['pr'], w=[K('E2v')])
                        p.op('dve', lambda e: e.tensor_copy(out=clv, in_=PR[:, :].rearrange("p (h l) -> p h l", h=4)[:, :, 127]), r=['pr'], w=[K('clv')])
                        p.op('act', lambda e: e.activation(out=Ev, in_=Ev, func=AF.Exp), r=[K('Ev')], w=[K('Ev')])
                        p.op('pool', lambda e: e.tensor_tensor(out=MT, in0=Ev, in1=CBm.unsqueeze(1).to_broadcast([128, 4, 128]), op=ALU.mult), r=[K('Ev'), K('CBm')], w=[K('MT')])
                        p.op('pool', lambda e: e.tensor_tensor(out=CsT, in0=E2v, in1=ft[:, 3, ck].unsqueeze(1).to_broadcast([128, 4, 128]), op=ALU.mult),
                             r=[K('E2v'), ('ft', 3)], w=[K('CsT')])

                    def back(c, ci, tb=tb, g=g, csv=csv, KD=KD):
                        ck = slice(c * 128, (c + 1) * 128)
                        B_tok, MT, CsT, xdp, clv = B_tok2[ci], MT2[ci], CsT2[ci], xdp2[ci], clv2[ci]
                        K = lambda n: (n, ci)
                        for pr in range(2):
                            n = 0
                            for h in (2 * pr, 2 * pr + 1):
                                p.op('pe', lambda e, h=h, pr=pr, n=n: e.matmul(PY[:, pr * 128:(pr + 1) * 128], lhsT=xdp[:, h, :], rhs=MT[:, h, :], start=(n == 0), stop=False),
                                     r=[K('xdp'), K('MT')], w=['py'])
                                n += 1
                                p.op('pe', lambda e, h=h, pr=pr, n=n: e.matmul(PY[:, pr * 128:(pr + 1) * 128], lhsT=hTp[:, h, :], rhs=CsT[:, h, :], start=False, stop=(n == 3)),
                                     r=['hTp', K('CsT')], w=['py'])
                                n += 1
                            p.op('dve', lambda e, pr=pr: e.scalar_tensor_tensor(out=Y[:, pr, ck], in0=ft[:, pr, ck], scalar=dsk_s[:, g * 2 + pr:g * 2 + pr + 1],
                                                                            in1=PY[:, pr * 128:(pr + 1) * 128], op0=ALU.mult, op1=ALU.add),
                                 r=[('ft', pr), 'py', 'sm'], w=[('Y', pr)])
                        p.op('dve', lambda e: e.tensor_tensor(out=dtmp, in0=clv, in1=csv[:, c * 4:(c + 1) * 4], op=ALU.subtract), r=[K('clv'), KD('csv')], w=['dtmp'])
                        p.op('act', lambda e: e.activation(out=dtev, in_=dtmp, func=AF.Exp), r=['dtmp'], w=['dtev'])
                        p.op('act', lambda e: e.activation(out=cdv, in_=clv, func=AF.Exp), r=[K('clv')], w=['cdv'])
                        for h in range(4):
                            hs = slice((h % 2) * 64, (h % 2) * 64 + 64)
                            p.op('dve', lambda e, h=h, hs=hs: e.tensor_scalar(out=xdd[:, h * 64:(h + 1) * 64], in0=xdp[:, h, hs], scalar1=dtev[:, h:h + 1], scalar2=None, op0=ALU.mult),
                                 r=[K('xdp'), 'dtev'], w=['xdd'])
                        p.op('pe', lambda e: e.matmul(PCB[:, 128:384], lhsT=B_tok, rhs=xdd, start=True, stop=True), r=[K('B_tok'), 'xdd'], w=['pst'])
                        for h in range(4):
                            hs = slice((h % 2) * 64, (h % 2) * 64 + 64)
                            p.op('dve', lambda e, h=h: e.scalar_tensor_tensor(out=hT[:, h * 64:(h + 1) * 64], in0=hT[:, h * 64:(h + 1) * 64], scalar=cdv[:, h:h + 1],
                                                                          in1=PCB[:, 128 + h * 64:128 + (h + 1) * 64], op0=ALU.mult, op1=ALU.add),
                                 r=['hT', 'cdv', 'pst'], w=['hT'])
                            p.op('pool', lambda e, h=h, hs=hs: e.tensor_copy(out=hTp[:, h, hs], in_=hT[:, h * 64:(h + 1) * 64]), r=['hT'], w=['hTp'])

                    front(0, ch_i % 2)
                    for c in range(4):
                        ci = ch_i % 2
                        ch_i += 1
                        if c + 1 < 4:
                            front(c + 1, ch_i % 2)
                        back(c, ci)
                    for pr in range(2):
                        p.op('pool', lambda e, pr=pr: e.tensor_tensor(out=Y[:, pr, :], in0=Y[:, pr, :], in1=zs[:, pr, :], op=ALU.mult), r=[('Y', pr), ('zs', pr)], w=[('Y', pr)])
                        p.op('pool', lambda e, pr=pr: e.tensor_tensor(out=sqy[:, pr, :], in0=Y[:, pr, :], in1=Y[:, pr, :], op=ALU.mult), r=[('Y', pr)], w=[('sqy', pr)])
                    for pr in range(2):
                        p.op('pe', lambda e, pr=pr: e.matmul(PN[:, :], lhsT=onesb[:, 128:256], rhs=sqy[:, pr, :], start=(pr == 0), stop=(pr == 1)),
                             r=[('sqy', pr), 'c1'], w=['pn'])
                    p.op('act', lambda e: e.activation(out=tmpn, in_=PN[:, :], func=AF.Sqrt, bias=epsb), r=['pn', 'c1'], w=[('acc', 0)])
                    p.op('dve', lambda e: e.reciprocal(out=rsn, in_=tmpn), r=[('acc', 0)], w=[('acc', 1)])
                    for pr in range(2):
                        p.op('dve', lambda e, pr=pr, g=g: e.scalar_tensor_tensor(out=ob[:, pr, :], in0=Y[:, pr, :], scalar=ssdn_s[:, g * 2 + pr:g * 2 + pr + 1], in1=rsn,
                                                                        op0=ALU.mult, op1=ALU.mult), r=[('Y', pr), ('acc', 1), 'sm'], w=[('ob', pr)])
                        r0 = g * 256 + pr * 128
                        dma(mix_dst(r0, 128, blk), ob[:, pr, :], [('ob', pr)], [('mo', len(p.ops))])
                        mo_keys[r0 // 128].append(('mo', len(p.ops) - 1))
            p.barrier()

        emit_casts(1000)
        if 'f0' in stages:
            ar.reset()
            wf_f = ar.f32([128, 8, 8])
            wf_b = ar.bf16([128, 8, 8])
            ones8 = ar.f32([128, 512])
            carry = ar.f32([128, 1])
            EV = [ar.f32([128, 512]) for _ in range(2)]
            SP = [ar.f32([128, 512]) for _ in range(2)]
            CSB = [ar.f32([128, 512]) for _ in range(2)]
            R1 = [ar.f32([128, 512]) for _ in range(2)]
            R2 = [ar.f32([128, 512]) for _ in range(2)]
            HI = [ar.bf16([128, 512]) for _ in range(2)]
            MID = [ar.bf16([128, 512]) for _ in range(2)]
            LO = [ar.bf16([128, 512]) for _ in range(2)]
            dma(wf_f, wf.rearrange("(k p) c -> p k c", p=128), [], ['wf_f'])
            p.op('pool', lambda e: e.tensor_copy(out=wf_b, in_=wf_f), r=['wf_f'], w=['wf_b'])
            p.op('pool', lambda e: e.memset(ones8, 1.0), w=['ones8'])
            for tb in range(NB):
                i = tb % 2
                blk = slice(tb * 512, (tb + 1) * 512)
                pj = PB[i]
                for k in range(8):
                    p.op('pe', lambda e, pj=pj, k=k, blk=blk: e.matmul(pj[0:8, :], lhsT=wf_b[:, k, :], rhs=uT[:, k, blk], start=(k == 0), stop=(k == 7)),
                         r=['wf_b', ('uT', tb)], w=[('pj', i)])
                ev, spv, csb, r1, r2, hi, mid, lo = EV[i][0:8, :], SP[i][0:8, :], CSB[i][0:8, :], R1[i][0:8, :], R2[i][0:8, :], HI[i][0:8, :], MID[i][0:8, :], LO[i][0:8, :]
                p.op('act', lambda e, pj=pj, ev=ev: e.activation(out=ev, in_=pj[0:8, :], func=AF.Exp, bias=nfb[0:8, :], scale=-1.0), r=[('pj', i), 'nfb'], w=[('ev', i)])
                p.op('act', lambda e, ev=ev, spv=spv: e.activation(out=spv, in_=ev, func=AF.Ln, bias=oneb[0:8, :]), r=[('ev', i), 'c1'], w=[('sp', i)])
                init = 0.0 if tb == 0 else carry[0:8, :]
                p.op('dve', lambda e, csb=csb, spv=spv, init=init: e.tensor_tensor_scan(out=csb, data0=ones8[0:8, :], data1=spv, initial=init, op0=ALU.mult, op1=ALU.add),
                     r=[('sp', i), 'ones8', 'carry'], w=[('csb', i)])
                p.op('dve', lambda e, csb=csb: e.tensor_copy(out=carry[0:8, :], in_=csb[:, 511:512]), r=[('csb', i)], w=['carry'])
                p.op('act', lambda e, hi=hi, csb=csb: e.activation(out=hi, in_=csb, func=AF.Copy, scale=-1.0), r=[('csb', i)], w=[('hi', i)])
                p.op('dve', lambda e, r1=r1, csb=csb, hi=hi: e.tensor_tensor(out=r1, in0=csb, in1=hi, op=ALU.add), r=[('csb', i), ('hi', i)], w=[('r1', i)])
                p.op('act', lambda e, mid=mid, r1=r1: e.activation(out=mid, in_=r1, func=AF.Copy, scale=-1.0), r=[('r1', i)], w=[('mid', i)])
                p.op('dve', lambda e, r2=r2, r1=r1, mid=mid: e.tensor_tensor(out=r2, in0=r1, in1=mid, op=ALU.add), r=[('r1', i), ('mid', i)], w=[('r2', i)])
                p.op('act', lambda e, lo=lo, r2=r2: e.activation(out=lo, in_=r2, func=AF.Copy, scale=-1.0), r=[('r2', i)], w=[('lo', i)])
                dma(crow[0, :, blk], hi, [('hi', i)], ['crow'])
                dma(crow[1, :, blk], mid, [('mid', i)], ['crow'])
                dma(crow[2, :, blk], lo, [('lo', i)], ['crow'])
                for c in range(4):
                    p.op('pe', lambda e, c=c, csb=csb: e.transpose(out=PB[2][:, c * 8:(c + 1) * 8], in_=csb[:, c * 128:(c + 1) * 128], identity=identf[0:8, 0:8]),
                         r=[('csb', i), 'cf'], w=['pt8'])
                p.op('dve', lambda e, tb=tb: e.tensor_copy(out=ncol[:, tb * 4:(tb + 1) * 4, :], in_=PB[2][:, 0:32].rearrange("p (c h) -> p c h", c=4)), r=['pt8'], w=['ncol'])
            p.barrier()

        if 'fox' in stages:
            ar.reset()
            wst = ar.f32([128, 8, 192])
            wh = ar.bf16([128, 8, 192])
            wzst = wst[:, :, 0:128]
            wzb = ar.bf16([128, 8, 128])
            K_aug = ar.bf16([128, T])
            V_aug = ar.bf16([128, NCH, 128])
            Q_aug = [ar.bf16([128, 512]) for _ in range(2)]
            PTt = [ar.bf16([128, 1024]) for _ in range(3)]
            OC = [ar.f32([128, 512]) for _ in range(2)]
            rl = ar.f32([128, 512])
            tmpo = ar.f32([128, 512])
            szf = [ar.bf16([128, 512]) for _ in range(2)]
            obf = [ar.bf16([128, 512]) for _ in range(2)]
            EZ = [ar.f32([128, 512]) for _ in range(2)]
            ST = [PP[0], PP[1], PP[2]]
            OB = [PB[6], PB[7]]
            p.op('pool', lambda e: e.memset(K_aug[64:128, :], 0.0), w=['K_aug'])
            p.op('pool', lambda e: e.memset(K_aug[64:67, :], 1.0), w=['K_aug'])
            p.op('pool', lambda e: e.memset(V_aug[:, :, 64:128], 1.0), w=['V_aug'])
            for qi in range(2):
                p.op('pool', lambda e, qi=qi: e.memset(Q_aug[qi][64:128, :], 0.0), w=[('qa', qi)])
                dma(Q_aug[qi][67:70, :], cneg, [], [('qa', qi)])
            st_i = 0
            pt_i = 0
            qb_i = 0

            def next_st():
                nonlocal st_i
                r_ = (ST[st_i % 3], ('st', st_i % 3))
                st_i += 1
                return r_
            for h in range(8):
                dma(wst, wfox[h].rearrange("(k p) c -> p k c", p=128), [], ['wst'])
                p.op('pool', lambda e: e.tensor_copy(out=wh, in_=wst), r=['wst'], w=['wh'])
                if h % 2 == 0:
                    dma(wzst, wzf[:, (h // 2) * 128:(h // 2 + 1) * 128].rearrange("(k p) c -> p k c", p=128), [], ['wst'])
                    p.op('pool', lambda e: e.tensor_copy(out=wzb, in_=wzst), r=['wst'], w=['wzb'])
                dma(K_aug[67:70, :], crow[:, h, :], ['crow'], ['K_aug'])
                for tb in range(NB):
                    blk = slice(tb * 512, (tb + 1) * 512)
                    pj, pk = next_st()
                    for k in range(8):
                        p.op('pe', lambda e, pj=pj, k=k, blk=blk: e.matmul(pj[0:64, 0:512], lhsT=wh[:, k, 64:128], rhs=uT[:, k, blk], start=(k == 0), stop=(k == 7)),
                             r=['wh', ('uT', tb)], w=[pk])
                    p.op('dve', lambda e, pj=pj, blk=blk: e.tensor_copy(out=K_aug[0:64, blk], in_=pj[0:64, 0:512]), r=[pk], w=['K_aug'])
                for cb8 in range(NCH // 8):
                    pj, pk = next_st()
                    for c in range(8):
                        tok = slice((cb8 * 8 + c) * 128, (cb8 * 8 + c + 1) * 128)
                        for k in range(8):
                            p.op('pe', lambda e, pj=pj, c=c, k=k, tok=tok: e.matmul(pj[:, (c // 4) * 512 + (c % 4) * 64:(c // 4) * 512 + (c % 4) * 64 + 64], lhsT=uT[:, k, tok], rhs=wh[:, k, 128:192],
                                                                                    start=(k == 0), stop=(k == 7)),
                                 r=['wh', ('uT', cb8 * 2 + c // 4)], w=[pk])
                    for hb in range(2):
                        p.op('dve', lambda e, pj=pj, cb8=cb8, hb=hb: e.tensor_copy(out=V_aug[:, cb8 * 8 + hb * 4:cb8 * 8 + hb * 4 + 4, 0:64],
                                                                                 in_=pj[:, hb * 512:hb * 512 + 256].rearrange("p (c d) -> p c d", c=4)),
                             r=[pk], w=['V_aug'])
                half = slice((h % 2) * 64, (h % 2) * 64 + 64)

                def prologue(qb, qi, h=h):
                    blk = slice(qb * 512, (qb + 1) * 512)
                    qa = Q_aug[qi]
                    pj, pk = next_st()
                    for k in range(8):
                        p.op('pe', lambda e, pj=pj, k=k, blk=blk: e.matmul(pj[0:64, 0:512], lhsT=wh[:, k, 0:64], rhs=uT[:, k, blk], start=(k == 0), stop=(k == 7)),
                             r=['wh', ('uT', qb)], w=[pk])
                    for k in range(8):
                        p.op('pe', lambda e, pj=pj, k=k, blk=blk: e.matmul(pj[:, 512:1024], lhsT=wzb[:, k, :], rhs=uT[:, k, blk], start=(k == 0), stop=(k == 7)),
                             r=['wzb', ('uT', qb)], w=[pk])
                    p.op('dve', lambda e, pj=pj, qa=qa: e.tensor_scalar(out=qa[0:64, :], in0=pj[0:64, 0:512], scalar1=0.125, scalar2=None, op0=ALU.mult), r=[pk], w=[('qa', qi)])
                    dma(qa[64:67, :], crow[:, h, blk], ['crow'], [('qa', qi)])
                    ez = EZ[qi]
                    p.op('act', lambda e, pj=pj, ez=ez: e.activation(out=ez, in_=pj[:, 512:1024], func=AF.Exp, scale=-1.0), r=[pk], w=[('ez', qi)])
                    p.op('dve', lambda e, ez=ez: e.tensor_scalar(out=ez, in0=ez, scalar1=1.0, scalar2=None, op0=ALU.add), r=[('ez', qi)], w=[('ez', qi)])
                    p.op('dve', lambda e, ez=ez: e.reciprocal(out=ez, in_=ez), r=[('ez', qi)], w=[('ez', qi)])
                    sz = szf[qi]
                    p.op('dve', lambda e, pj=pj, ez=ez, sz=sz: e.tensor_tensor(out=sz, in0=pj[:, 512:1024], in1=ez, op=ALU.mult), r=[pk, ('ez', qi)], w=[('sz', qi)])

                def finalize2(qb, qi, h=h, half=half):
                    blk = slice(qb * 512, (qb + 1) * 512)
                    oc = OC[qi]
                    ob_ps = OB[qi]
                    ok = ('ob', qi)
                    sz = szf[qi]
                    of = obf[qi]
                    p.op('pe', lambda e: e.matmul(ob_ps, lhsT=swapm, rhs=oc, start=True, stop=True), r=['cf', ('oc', qi)], w=[ok])
                    if h % 2 == 0:
                        p.op('dve', lambda e: e.tensor_tensor(out=tmpo[0:64, :], in0=oc[0:64, :], in1=ob_ps[0:64, :], op=ALU.mult), r=[('oc', qi), ok], w=['tmpo'])
                        p.op('pool', lambda e: e.tensor_tensor(out=of[0:64, :], in0=tmpo[0:64, :], in1=sz[0:64, :], op=ALU.mult), r=['tmpo', ('sz', qi)], w=[('of', qi)])
                    else:
                        p.op('dve', lambda e: e.tensor_tensor(out=tmpo[64:128, :], in0=ob_ps[64:128, :], in1=rl[64:128, :], op=ALU.mult), r=['rl', ok], w=['tmpo'])
                        p.op('pool', lambda e: e.tensor_tensor(out=of[64:128, :], in0=tmpo[64:128, :], in1=sz[64:128, :], op=ALU.mult), r=['tmpo', ('sz', qi)], w=[('of', qi)])
                    r0 = 512 + h * 64
                    dma(mix_dst(r0, 64, blk), of[half, :], [('of', qi)], [('mo', len(p.ops))])
                    mo_keys[r0 // 128].append(('mo', len(p.ops) - 1))

                prologue(0, qb_i % 2)
                deferred = None
                for qb in range(NB):
                    qi = qb_i % 2
                    qb_i += 1
                    qa = Q_aug[qi]
                    nk = 4 * qb + 4
                    ob_ps = OB[qi]
                    ok = ('ob', qi)
                    pend = None
                    for kp in range(nk // 2):
                        st, sk = next_st()
                        for hh in range(2):
                            kc = 2 * kp + hh
                            diag = kc >= 4 * qb
                            ks = slice(kc * 128, (kc + 1) * 128)
                            dst = slice(hh * 512, hh * 512 + 512)
                            p.op('pe', lambda e, st=st, ks=ks, qa=qa, diag=diag, dst=dst: e.matmul(st[:, dst], lhsT=K_aug[:, ks], rhs=qa[:, :], start=True, stop=(not diag)),
                                 r=['K_aug', ('qa', qi)], w=[sk])
                            if diag:
                                j = kc - 4 * qb
                                p.op('pe', lambda e, st=st, j=j, dst=dst: e.matmul(st[:, dst], lhsT=identb, rhs=maskJ[:, j, :], start=False, stop=True), r=['cb'], w=[sk])
                        if pend is not None:
                            pend()
                        pt = PTt[pt_i % 3]
                        ptk = ('pt', pt_i % 3)
                        pt_i += 1
                        p.op('act', lambda e, st=st, pt=pt: e.activation(out=pt, in_=st[:, :], func=AF.Exp), r=[sk], w=[ptk])

                        def mk(pt=pt, ptk=ptk, kp=kp, nk=nk, ob_ps=ob_ps, ok=ok):
                            for hh in range(2):
                                kc = 2 * kp + hh
                                p.op('pe', lambda e, kc=kc, hh=hh: e.matmul(ob_ps, lhsT=V_aug[:, kc, :], rhs=pt[:, hh * 512:hh * 512 + 512], start=(kc == 0), stop=(kc == nk - 1)),
                                     r=['V_aug', ptk], w=[ok])
                        pend = mk
                        if kp == min(3, nk // 2 - 1):
                            if deferred is not None:
                                deferred()
                                deferred = None
                            if qb + 1 < NB:
                                prologue(qb + 1, qb_i % 2)
                    pend()
                    oc = OC[qi]
                    p.op('dve', lambda e, oc=oc, ob_ps=ob_ps: e.tensor_copy(out=oc, in_=ob_ps), r=[ok], w=[('oc', qi)])
                    if h % 2 == 0:
                        p.op('dve', lambda e, oc=oc: e.reciprocal(out=oc[64:128, :], in_=oc[64:128, :]), r=[('oc', qi)], w=[('oc', qi)])
                    else:
                        p.op('dve', lambda e, oc=oc: e.reciprocal(out=rl[64:128, :], in_=oc[64:128, :]), r=[('oc', qi)], w=['rl'])
                    deferred = (lambda qb=qb, qi=qi: finalize2(qb, qi))
                deferred()
        if fused:
            p.op('act', lambda e: e.activation(out=sm[:, 130:131], in_=sm[:, 25:26], func=AF.Copy), r=['c1'], w=[('fin', 'act')])
            p.op('dve', lambda e: e.tensor_copy(out=sm[:, 131:132], in_=sm[:, 25:26]), r=['c1'], w=[('fin', 'dve')])
            p.op('pool', lambda e: e.memset(sm[:, 132:133], 0.0), w=[('fin', 'pool')])
            for k in range(8):
                ml, Gk = fused['mixloc'][k], fused['G'][k]
                p.op('pool', lambda e, ml=ml, Gk=Gk: e.collective_compute("AllGather", ALU.bypass, replica_groups=[[0, 1], [2, 3], [4, 5], [6, 7]],
                                                                          ins=[ml.ap().opt()], outs=[Gk.ap().opt()]),
                     r=mo_keys[k] + [('fin', 'act'), ('fin', 'dve'), ('fin', 'pool')], w=[('G', k)], dma=True, cc=True)
            p.run(es, sems=fused['semsA'])
            fused['finalA'] = p.final_values()
        else:
            p.run(es)
    return nc


def build_B(nc, TO, fused=None):
    TB = TO + 128
    D = lambda name, shape, dt, kind=None: (nc.dram_tensor(name, shape, dt, kind=kind).ap() if kind else nc.dram_tensor(name, shape, dt).ap())
    if fused:
        Gbf = fused["G_bf"]
        halfS = fused["half"]
        xT = D("xTB", [1024, TB], F32, "ExternalInput")
        mseld = D("msel", [128, 2], F32, "ExternalInput")
    else:
        mixT = D("mixT", [2048, TB], BF16, "ExternalInput")
        xT = D("xT", [1024, TB], F32, "ExternalInput")
    if not fused:
        w_out0 = D("w_out0", [2048, 1024], F32, "ExternalInput")
        w_in1 = D("w_in1", [1024, 6144], F32, "ExternalInput")
        w_out1 = D("w_out1", [2048, 1024], F32, "ExternalInput")
    vecs = D("vecs", [128, 88], F32, "ExternalInput")
    cw1 = D("cw1", [128, 16, 31], F32, "ExternalInput")
    yT = D("yT", [1024, TO], F32, "ExternalOutput")
    if fused:
        s_out0, s_in1, s_out1 = fused['s_out0'], fused['s_in1'], fused['s_out1']
    else:
        s_out0 = D("s_out0", [8, 128, 16, 128], BF16)
        s_in1 = D("s_in1", [48, 128, 8, 128], BF16)
        s_out1 = D("s_out1", [8, 128, 16, 128], BF16)
    s_dg = fused["s_dg"] if fused else D("s_dg", [16, 128, 31 * 128], BF16)
    cbdB = D("cbdB", [128, 128], BF16, "ExternalInput")

    with ExitStack() as es:
        sb = lambda n, s, d: es.enter_context(nc.sbuf_tensor("b_" + n, s, d))
        ps = lambda n, s, d: es.enter_context(nc.psum_tensor("b_" + n, s, d))
        vec = sb("vec", [128, 88], F32)
        msel = sb("msel", [128, 2], F32)
        gpost0 = vec[:, 0:8]
        gpre1 = vec[:, 8:16]
        gpost1 = vec[:, 16:24]
        cb1 = vec[:, 24:40]
        lng = vec[:, 40:56]
        lnb = vec[:, 56:72]
        epsb = vec[:, 72:73]
        cw = sb("cw", [128, 16, 31], F32)
        onesb = sb("onesb", [128, 256], BF16)
        dgb = [sb("dgb%d" % i, [128, 31 * 128], BF16) for i in range(2)]
        wst = [dgb[i][:, 0:2048].bitcast(F32) for i in range(2)]
        wcb = [dgb[i][:, 2048:3072] for i in range(2)]
        identb = sb("identb", [128, 128], BF16)
        mx = sb("mx", [128, 16, 512], BF16)
        xb = sb("xb", [128, 8, 512], F32)
        o0 = sb("o0", [128, 8, 512], F32)
        sq = sb("sq", [128, 8, 512], BF16)
        tmp = sb("tmp", [128, 512], F32)
        rs = sb("rs", [128, 512], F32)
        u1 = sb("u1", [128, 8, 512], BF16)
        wo = [sb("wo%d" % i, [128, 16, 128], BF16) for i in range(3)]
        wi = [sb("wi%d" % i, [128, 8, 128], BF16) for i in range(4)]
        sig = [sb("sig%d" % i, [128, 512], F32) for i in range(2)]
        hbuf = sb("hbuf", [128, 16, 542], BF16)
        cbuf = sb("cbuf", [128, 16, 512], F32)
        c16 = [sb("c16_%d" % i, [128, 512], BF16) for i in range(2)]
        csq = [sb("csq%d" % i, [128, 512], BF16) for i in range(2)]
        mean = sb("mean", [128, 512], F32)
        var = sb("var", [128, 512], F32)
        rln = sb("rln", [128, 512], F32)
        t1 = [sb("t1_%d" % i, [128, 512], F32) for i in range(2)]
        s1 = [sb("s1_%d" % i, [128, 512], F32) for i in range(2)]
        szb = [sb("szb%d" % i, [128, 512], F32) for i in range(2)]
        h2 = sb("h2", [128, 16, 512], BF16)
        yo = [sb("yo%d" % i, [128, 512], F32) for i in range(2)]
        PB = [ps("pb%d" % i, [128, 512], F32) for i in range(8)]

        p = Prog(nc)
        dma = lambda out, in_, r, w: p.op('sp', lambda e: e.dma_start(out=out, in_=in_), r=r, w=w, dma=True)
        dma(vec[:], vecs, [], ['vec'])
        dma(cw[:], cw1, [], ['cw'])
        if fused:
            dma(msel[:], mseld, [], ['msel'])
        p.op('pool', lambda e: e.memset(epsb, EPS), r=['vec'], w=['vec2'])
        p.op('pool', lambda e: e.memset(onesb[:, 0:128], 1.0 / 1024), w=['c1'])
        p.op('pool', lambda e: e.memset(onesb[:, 128:256], 1.0 / 2048), w=['c1'])
        p.op('pool', lambda e: e.memset(hbuf[:, :, 0:30], 0.0), w=[('hb', j) for j in range(16)])

        if not fused:
            wi_ = 0
            pieces = []
            for ec in range(16):
                pieces.append((w_out0[ec * 128:(ec + 1) * 128, :], s_out0[:, :, ec, :].rearrange("d p c -> p d c"), 's_out0'))
            for kc in range(8):
                for cc in range(6):
                    pieces.append((w_in1[kc * 128:(kc + 1) * 128, cc * 1024:(cc + 1) * 1024], s_in1[cc * 8:(cc + 1) * 8, :, kc, :].rearrange("d p c -> p d c"), 's_in1'))
            for ec in range(16):
                pieces.append((w_out1[ec * 128:(ec + 1) * 128, :], s_out1[:, :, ec, :].rearrange("d p c -> p d c"), 's_out1'))
            for n, (src, dst, key) in enumerate(pieces):
                i = n % 2
                dma(wst[i], src, [], [('wst', i)])
                eng = 'pool' if n % 3 else 'dve'
                p.op(eng, lambda e, i=i: e.tensor_copy(out=wcb[i], in_=wst[i]), r=[('wst', i)], w=[('wcb', i)])
                dma(dst, wcb[i].rearrange("p (d c) -> p d c", d=8), [('wcb', i)], [key])
        p.barrier()
        dma(identb[:], cbdB, [], ['identb'])
        for j in range(0 if fused else 16):
            i = j % 2
            for s_ in range(31):
                eng = 'pool' if s_ % 2 else 'dve'
                p.op(eng, lambda e, i=i, j=j, s_=s_: e.tensor_scalar(out=dgb[i][:, s_ * 128:(s_ + 1) * 128], in0=identb[:], scalar1=cw[:, j, s_:s_ + 1], scalar2=None, op0=ALU.mult),
                     r=['identb', 'cw'], w=[('dgw', i, s_)])
            dma(s_dg[j], dgb[i][:], [('dgw', i, s_) for s_ in range(31)], ['s_dg'])
        p.barrier()

        cnt = dict(pj=0, wo=0, wi=0, aux=0)
        PJ = PB[0:3]
        PCV = [PB[3], PB[7]]
        PST = PB[4]
        PMEAN = PB[5]
        PMSQ = PB[6]

        def proj_out(src, skey, sname, W, dst_fn):
            for dj in range(8):
                wt = wo[cnt['wo'] % 3]
                wk = ('wo', cnt['wo'] % 3)
                cnt['wo'] += 1
                sc = s_out0 if sname == 's_out0' else s_out1
                dma(wt[:], sc[dj], [sname], [wk])
                pj = PJ[cnt['pj'] % 3]
                pk = ('pj', cnt['pj'] % 3)
                cnt['pj'] += 1
                for k in range(16):
                    p.op('pe', lambda e, pj=pj, wt=wt, k=k: e.matmul(pj[:, 0:W], lhsT=wt[:, k, :], rhs=src[:, k, 0:W], start=(k == 0), stop=(k == 15)),
                         r=[wk] + skey, w=[pk])
                dst_fn(dj, pj, pk)

        def rms_stats(W, srckey):
            for k in range(8):
                p.op('pe', lambda e, k=k: e.matmul(PST[:, 0:W], lhsT=onesb[:, 0:128], rhs=sq[:, k, 0:W], start=(k == 0), stop=(k == 7)), r=[('sq', k), 'c1'], w=['pst'])
            p.op('act', lambda e: e.activation(out=tmp[:, 0:W], in_=PST[:, 0:W], func=AF.Sqrt, bias=epsb), r=['pst', 'vec2'], w=['tmp'])
            p.op('dve', lambda e: e.reciprocal(out=rs[:, 0:W], in_=tmp[:, 0:W]), r=['tmp'], w=['rs'])

        blocks = [(0, 128)] + [(128 + 512 * i, 512) for i in range(TO // 512)]
        def load_mix(bi, s0, W):
            halo = (bi == 0)
            MXA = [('mx', k_) for k_ in range(8)]
            H2A = [('h2', k_) for k_ in range(8)]
            tb0 = halfS + s0 - 128
            for k in range(8):
                dma(h2[:, 2 * k:2 * k + 2, 0:W], Gbf[k][:, tb0:tb0 + W].rearrange("(r p) t -> p r t", p=128), [('G', k)], [('h2', k)])
            if halo:
                p.op('dve', lambda e: e.tensor_scalar(out=mx[:, :, 0:W], in0=h2[:, :, 0:W], scalar1=msel[:, 1:2], scalar2=None, op0=ALU.mult), r=H2A + ['msel'], w=MXA)
            else:
                ta0 = s0 - 128
                for k in range(8):
                    dma(mx[:, 2 * k:2 * k + 2, 0:W], Gbf[k][:, ta0:ta0 + W].rearrange("(r p) t -> p r t", p=128), [('G', k)], [('mx', k)])
                p.op('dve', lambda e: e.tensor_scalar(out=mx[:, :, 0:W], in0=mx[:, :, 0:W], scalar1=msel[:, 0:1], scalar2=None, op0=ALU.mult), r=MXA + ['msel'], w=MXA)
                p.op('dve', lambda e: e.scalar_tensor_tensor(out=mx[:, :, 0:W], in0=h2[:, :, 0:W], scalar=msel[:, 1:2], in1=mx[:, :, 0:W], op0=ALU.mult, op1=ALU.add),
                     r=H2A + MXA + ['msel'], w=MXA)

        def do_block(bi, s0, W):
            halo = (bi == 0)
            if not fused:
                dma(mx[:, :, 0:W], mixT[:, s0:s0 + W].rearrange("(k p) t -> p k t", p=128), [], [('mx', k_) for k_ in range(8)])
            elif bi == 0:
                load_mix(bi, s0, W)
            dma(xb[:, :, 0:W], xT[:, s0:s0 + W].rearrange("(k p) t -> p k t", p=128), [], [('xb', k) for k in range(8)])

            def ev0(dj, pj, pk):
                p.op('act', lambda e: e.activation(out=o0[:, dj, 0:W], in_=pj[:, 0:W], func=AF.Copy), r=[pk], w=[('o0', dj)])
                p.op('pool', lambda e: e.tensor_tensor(out=sq[:, dj, 0:W], in0=o0[:, dj, 0:W], in1=o0[:, dj, 0:W], op=ALU.mult), r=[('o0', dj)], w=[('sq', dj)])
            proj_out(mx, [('mx', k_) for k_ in range(8)], 's_out0', W, ev0)
            rms_stats(W, None)
            for k in range(8):
                p.op('dve', lambda e, k=k: e.scalar_tensor_tensor(out=o0[:, k, 0:W], in0=o0[:, k, 0:W], scalar=gpost0[:, k:k + 1], in1=rs[:, 0:W], op0=ALU.mult, op1=ALU.mult),
                     r=[('o0', k), 'rs', 'vec'], w=[('o0', k)])
                p.op('dve', lambda e, k=k: e.tensor_tensor(out=xb[:, k, 0:W], in0=xb[:, k, 0:W], in1=o0[:, k, 0:W], op=ALU.add), r=[('o0', k), ('xb', k)], w=[('xb', k)])
                p.op('pool', lambda e, k=k: e.tensor_tensor(out=sq[:, k, 0:W], in0=xb[:, k, 0:W], in1=xb[:, k, 0:W], op=ALU.mult), r=[('xb', k)], w=[('sq', k)])
            rms_stats(W, None)
            for k in range(8):
                p.op('dve', lambda e, k=k: e.scalar_tensor_tensor(out=u1[:, k, 0:W], in0=xb[:, k, 0:W], scalar=gpre1[:, k:k + 1], in1=rs[:, 0:W], op0=ALU.mult, op1=ALU.mult),
                     r=[('xb', k), 'rs', 'vec'], w=['u1'])
            if fused and bi + 1 < len(blocks):
                load_mix(bi + 1, blocks[bi + 1][0], blocks[bi + 1][1])

            def inproj(col):
                wt = wi[cnt['wi'] % 4]
                wk = ('wi', cnt['wi'] % 4)
                cnt['wi'] += 1
                dma(wt[:], s_in1[col // 128], ['s_in1'], [wk])
                pj = PJ[cnt['pj'] % 3]
                pk = ('pj', cnt['pj'] % 3)
                cnt['pj'] += 1
                for k in range(8):
                    p.op('pe', lambda e, pj=pj, wt=wt, k=k: e.matmul(pj[:, 0:W], lhsT=wt[:, k, :], rhs=u1[:, k, 0:W], start=(k == 0), stop=(k == 7)), r=[wk, 'u1'], w=[pk])
                return pj, pk

            for j in range(16):
                pv, pvk = inproj(j * 128)
                pg, pgk = inproj(2048 + j * 128)
                sg = sig[j % 2]
                p.op('act', lambda e, pg=pg, sg=sg: e.activation(out=sg[:, 0:W], in_=pg[:, 0:W], func=AF.Sigmoid), r=[pgk], w=[('sig', j % 2)])
                p.op('dve', lambda e, pv=pv, sg=sg, j=j: e.tensor_tensor(out=hbuf[:, j, 30:30 + W], in0=pv[:, 0:W], in1=sg[:, 0:W], op=ALU.mult), r=[pvk, ('sig', j % 2)], w=[('hb', j)])
            if halo:
                for j in range(16):
                    eng = 'dve' if j % 2 == 0 else 'pool'
                    p.op(eng, lambda e, j=j: e.tensor_copy(out=hbuf[:, j, 0:30], in_=hbuf[:, j, W:W + 30]), r=[('hb', j)], w=[('hb', j)])
                return
            for j in range(16):
                acc = cbuf[:, j, 0:W]
                di = cnt['aux'] % 2
                cnt['aux'] += 1
                dg = dgb[di]
                pcv = PCV[di]
                p.op('pool', lambda e, dg=dg, j=j: e.dma_start(out=dg[:], in_=s_dg[j]), r=['s_dg'], w=[('dg', di)], dma=True)
                for s in range(31):
                    p.op('pe', lambda e, pcv=pcv, dg=dg, j=j, s=s: e.matmul(pcv[:, 0:W], lhsT=dg[:, s * 128:(s + 1) * 128], rhs=hbuf[:, j, s:s + W], start=(s == 0), stop=(s == 30)),
                         r=[('dg', di), ('hb', j)], w=[('pcv', di)])
                p.op('act', lambda e, acc=acc, pcv=pcv, j=j: e.activation(out=acc, in_=pcv[:, 0:W], func=AF.Identity, bias=cb1[:, j:j + 1]), r=[('pcv', di), 'vec'], w=[('cb', j)])
                p.op('pool', lambda e, j=j: e.tensor_copy(out=hbuf[:, j, 0:30], in_=hbuf[:, j, W:W + 30]), r=[('hb', j)], w=[('hb', j)])
                a = j % 2
                p.op('act', lambda e, acc=acc, a=a: e.activation(out=c16[a][:, 0:W], in_=acc, func=AF.Copy), r=[('cb', j)], w=[('c16', a)])
                p.op('act', lambda e, acc=acc, a=a: e.activation(out=csq[a][:, 0:W], in_=acc, func=AF.Square), r=[('cb', j)], w=[('csq', a)])
                p.op('pe', lambda e, a=a, j=j: e.matmul(PMEAN[:, 0:W], lhsT=onesb[:, 128:256], rhs=c16[a][:, 0:W], start=(j == 0), stop=(j == 15)), r=[('c16', a), 'c1'], w=['pmean'])
                p.op('pe', lambda e, a=a, j=j: e.matmul(PMSQ[:, 0:W], lhsT=onesb[:, 128:256], rhs=csq[a][:, 0:W], start=(j == 0), stop=(j == 15)), r=[('csq', a), 'c1'], w=['pmsq'])
            p.op('act', lambda e: e.activation(out=mean[:, 0:W], in_=PMEAN[:, 0:W], func=AF.Copy), r=['pmean'], w=['mean'])
            p.op('dve', lambda e: e.tensor_tensor(out=var[:, 0:W], in0=mean[:, 0:W], in1=mean[:, 0:W], op=ALU.mult), r=['mean'], w=['var'])
            p.op('dve', lambda e: e.tensor_tensor(out=var[:, 0:W], in0=PMSQ[:, 0:W], in1=var[:, 0:W], op=ALU.subtract), r=['pmsq', 'var'], w=['var'])
            p.op('act', lambda e: e.activation(out=tmp[:, 0:W], in_=var[:, 0:W], func=AF.Sqrt, bias=epsb), r=['var', 'vec2'], w=['tmp'])
            p.op('dve', lambda e: e.reciprocal(out=rln[:, 0:W], in_=tmp[:, 0:W]), r=['tmp'], w=['rln'])
            for j in range(16):
                a = j % 2
                pz, pzk = inproj(4096 + j * 128)
                p.op('act', lambda e, pz=pz, a=a: e.activation(out=szb[a][:, 0:W], in_=pz[:, 0:W], func=AF.Silu), r=[pzk], w=[('szb', a)])
                p.op('dve', lambda e, j=j, a=a: e.tensor_tensor(out=t1[a][:, 0:W], in0=cbuf[:, j, 0:W], in1=mean[:, 0:W], op=ALU.subtract), r=[('cb', j), 'mean'], w=[('t1', a)])
                p.op('dve', lambda e, a=a: e.tensor_tensor(out=t1[a][:, 0:W], in0=t1[a][:, 0:W], in1=rln[:, 0:W], op=ALU.mult), r=[('t1', a), 'rln'], w=[('t1', a)])
                p.op('act', lambda e, j=j, a=a: e.activation(out=s1[a][:, 0:W], in_=t1[a][:, 0:W], func=AF.Silu, bias=lnb[:, j:j + 1], scale=lng[:, j:j + 1]), r=[('t1', a), 'vec'], w=[('s1', a)])
                p.op('pool', lambda e, j=j, a=a: e.tensor_tensor(out=h2[:, j, 0:W], in0=s1[a][:, 0:W], in1=szb[a][:, 0:W], op=ALU.mult), r=[('s1', a), ('szb', a)], w=[('h2', j // 2)])
            proj_out(h2, [('h2', k_) for k_ in range(8)], 's_out1', W, ev0)
            rms_stats(W, None)
            for k in range(8):
                y = yo[k % 2]
                p.op('dve', lambda e, k=k, y=y: e.scalar_tensor_tensor(out=y[:, 0:W], in0=o0[:, k, 0:W], scalar=gpost1[:, k:k + 1], in1=rs[:, 0:W], op0=ALU.mult, op1=ALU.mult),
                     r=[('o0', k), 'rs', 'vec'], w=[('yo', k % 2)])
                p.op('dve', lambda e, k=k, y=y: e.tensor_tensor(out=y[:, 0:W], in0=y[:, 0:W], in1=xb[:, k, 0:W], op=ALU.add), r=[('yo', k % 2), ('xb', k)], w=[('yo', k % 2)])
                dma(yT[k * 128:(k + 1) * 128, s0 - 128:s0 - 128 + W], y[:, 0:W], [('yo', k % 2)], [])
        for bi_, (s0_, W_) in enumerate(blocks):
            do_block(bi_, s0_, W_)
        if fused:
            p.run(es, sems=fused['semsB'], pre_waits=fused['finalA'])
        else:
            p.run(es)
    return nc


def build_fused(nc, S):
    half = S // 2
    mixloc = [nc.dram_tensor("mixloc%d" % k, [128, S // 2], F32) for k in range(8)]
    G = [nc.dram_tensor("Gmix%d" % k, [256, S // 2], F32) for k in range(8)]
    with ExitStack() as es0:
        fused = dict(mixloc=mixloc, G=G, mixloc_bf=[m.ap().bitcast(BF16) for m in mixloc], G_bf=[g.ap().bitcast(BF16) for g in G], half=half)
        w_out0 = nc.dram_tensor("w_out0", [2048, 1024], F32, kind="ExternalInput").ap()
        w_in1 = nc.dram_tensor("w_in1", [1024, 6144], F32, kind="ExternalInput").ap()
        w_out1 = nc.dram_tensor("w_out1", [2048, 1024], F32, kind="ExternalInput").ap()
        s_out0 = nc.dram_tensor("s_out0", [8, 128, 16, 128], BF16).ap()
        s_in1 = nc.dram_tensor("s_in1", [48, 128, 8, 128], BF16).ap()
        s_out1 = nc.dram_tensor("s_out1", [8, 128, 16, 128], BF16).ap()
        fused.update(s_out0=s_out0, s_in1=s_in1, s_out1=s_out1)
        pieces = []
        for ec in range(16):
            pieces.append((w_out0[ec * 128:(ec + 1) * 128, :].rearrange("p (d c) -> p d c", d=8), s_out0[:, :, ec, :].rearrange("d p c -> p d c")))
        for kc in range(8):
            for cc in range(6):
                pieces.append((w_in1[kc * 128:(kc + 1) * 128, cc * 1024:(cc + 1) * 1024].rearrange("p (d c) -> p d c", d=8),
                               s_in1[cc * 8:(cc + 1) * 8, :, kc, :].rearrange("d p c -> p d c")))
        for ec in range(16):
            pieces.append((w_out1[ec * 128:(ec + 1) * 128, :].rearrange("p (d c) -> p d c", d=8), s_out1[:, :, ec, :].rearrange("d p c -> p d c")))
        dgf = nc.dram_tensor("dgf", [16, 128, 31 * 128], F32, kind="ExternalInput").ap()
        s_dg = nc.dram_tensor("s_dg", [16, 128, 31 * 128], BF16).ap()
        fused['s_dg'] = s_dg
        for j in range(16):
            pieces.append((dgf[j], s_dg[j]))
        fused['cast_pieces'] = pieces
        fused['semsA'] = Prog.make_sems(nc, es0, "a")
        fused['semsB'] = Prog.make_sems(nc, es0, "b")
        build_A(nc, S, fused=fused)
        build_B(nc, half, fused=fused)
    return nc


_PROGS = {}


def _get_prog(kind, T):
    key = (kind, T)
    if key not in _PROGS:
        nc = bass.Bass("TRN2", target_bir_lowering=False)
        if kind == 'A':
            build_A(nc, T)
        else:
            build_B(nc, T)
        _PROGS[key] = nc
    return _PROGS[key]


def _pk(v, n):
    return np.ascontiguousarray(np.asarray(v, np.float32).reshape(n, 128).T)


def _inputs_A(b, p, x, e_norm_pre, e_w_in, e_conv_w, e_conv_b, e_dt_bias, e_a_log, e_d_skip, e_fgate_b, e_ssd_norm, cf, cb):
    w = e_w_in[0]
    cwv = e_conv_w[0]
    cbv_ = e_conv_b[0]
    wssd = np.empty((2, 1024, 768), np.float32)
    cw = np.empty((128, 8, 4), np.float32)
    cbv = np.empty((128, 8), np.float32)
    dtb16 = np.empty((128, 2, 16), np.float32)
    alog16 = np.empty((128, 2, 16), np.float32)
    dsk = np.empty((128, 4), np.float32)
    ssdn = np.empty((128, 4), np.float32)
    pp = np.arange(128)
    for gl in range(2):
        G = 2 * p + gl
        wssd[gl, :, 0:256] = w[:, 256 * G:256 * G + 256]
        wssd[gl, :, 256:512] = w[:, 2048 + 256 * G:2048 + 256 * G + 256]
        wssd[gl, :, 512:640] = w[:, 3072 + 128 * G:3072 + 128 * G + 128]
        wssd[gl, :, 640:768] = w[:, 3584 + 128 * G:3584 + 128 * G + 128]
        chans = [256 * G + pp, 256 * G + 128 + pp, 1024 + 128 * G + pp, 1536 + 128 * G + pp]
        for j in range(4):
            cw[:, gl * 4 + j, :] = cwv[:, chans[j]].T
            cbv[:, gl * 4 + j] = cbv_[chans[j]]
        for c in range(4):
            dtb16[:, gl, c * 4:(c + 1) * 4] = e_dt_bias[0][4 * G:4 * G + 4][None, :]
            alog16[:, gl, c * 4:(c + 1) * 4] = e_a_log[0][4 * G:4 * G + 4][None, :]
        for pr in range(2):
            dsk[:, gl * 2 + pr] = e_d_skip[0][4 * G + (pr * 128 + pp) // 64]
            ssdn[:, gl * 2 + pr] = e_ssd_norm[0][256 * G + pr * 128 + pp]
    wfox = np.empty((8, 1024, 192), np.float32)
    for hl in range(8):
        H = 8 * p + hl
        wfox[hl, :, 0:64] = w[:, 4112 + 64 * H:4112 + 64 * H + 64]
        wfox[hl, :, 64:128] = w[:, 5136 + 64 * H:5136 + 64 * H + 64]
        wfox[hl, :, 128:192] = w[:, 6160 + 64 * H:6160 + 64 * H + 64]
    return dict(
        xT=np.ascontiguousarray(x[b].T), gpre=_pk(e_norm_pre[0], 8), wssd=wssd,
        wdt=np.ascontiguousarray(w[:, 4096 + 8 * p:4096 + 8 * p + 8]), cw=cw, cbv=cbv, dtb16=dtb16, alog16=alog16,
        dsk=dsk, ssdn=ssdn, wfox=wfox, wzf=np.ascontiguousarray(w[:, 1024 + 512 * p:1024 + 512 * p + 512]),
        wf=np.ascontiguousarray(w[:, 7184 + 8 * p:7184 + 8 * p + 8]),
        fb=np.ascontiguousarray(e_fgate_b[0][8 * p:8 * p + 8].reshape(8, 1)), cfd=cf, cbd=cb,
        cneg=np.full((3, 512), -1.0, ml_dtypes.bfloat16))


def kernel_unfused(x, e_norm_pre, e_w_in, e_conv_w, e_conv_b, e_dt_bias, e_a_log, e_d_skip, e_fgate_b,
           e_ssd_norm, e_w_out, e_norm_post, o_norm_pre, o_w_in, o_conv_w, o_conv_b, o_ln_g, o_ln_b,
           o_w_out, o_norm_post, _debug=None):
    args = [np.asarray(a, np.float32) for a in (x, e_norm_pre, e_w_in, e_conv_w, e_conv_b, e_dt_bias, e_a_log, e_d_skip, e_fgate_b,
                                                 e_ssd_norm, e_w_out, e_norm_post, o_norm_pre, o_w_in, o_conv_w, o_conv_b, o_ln_g, o_ln_b,
                                                 o_w_out, o_norm_post)]
    (x, e_norm_pre, e_w_in, e_conv_w, e_conv_b, e_dt_bias, e_a_log, e_d_skip, e_fgate_b,
     e_ssd_norm, e_w_out, e_norm_post, o_norm_pre, o_w_in, o_conv_w, o_conv_b, o_ln_g, o_ln_b, o_w_out, o_norm_post) = args
    Bn, S, Dm = x.shape
    half = S // 2
    cf, cb = _consts_np()
    ncA = _get_prog('A', S)
    mapsA = [_inputs_A(c // 2, c % 2, x, e_norm_pre, e_w_in, e_conv_w, e_conv_b, e_dt_bias, e_a_log, e_d_skip, e_fgate_b, e_ssd_norm, cf, cb)
             for c in range(8)]
    resA = run_bass_kernel_spmd(ncA, mapsA, core_ids=list(range(8)))
    mix = []
    for b in range(Bn):
        m0 = np.asarray(resA.results[2 * b]["mixT"])
        m1 = np.asarray(resA.results[2 * b + 1]["mixT"])
        mix.append(np.concatenate([m0[0:512], m1[0:512], m0[512:1024], m1[512:1024]], axis=0))
    if _debug is not None:
        _debug['mix'] = mix
    ncB = _get_prog('B', half)
    vecs = np.zeros((128, 88), np.float32)
    vecs[:, 0:8] = _pk(e_norm_post[0], 8)
    vecs[:, 8:16] = _pk(o_norm_pre[0], 8)
    vecs[:, 16:24] = _pk(o_norm_post[0], 8)
    vecs[:, 24:40] = _pk(o_conv_b[0], 16)
    vecs[:, 40:56] = _pk(o_ln_g[0], 16)
    vecs[:, 56:72] = _pk(o_ln_b[0], 16)
    cw1 = np.ascontiguousarray(o_conv_w[0].reshape(31, 16, 128).transpose(2, 1, 0))
    mapsB = []
    for c in range(8):
        b, p = c // 2, c % 2
        mT = np.zeros((2048, half + 128), ml_dtypes.bfloat16)
        xTt = np.zeros((1024, half + 128), np.float32)
        lo = half * p - 128
        if lo < 0:
            mT[:, 128:] = mix[b][:, 0:half]
            xTt[:, 128:] = x[b, 0:half].T
        else:
            mT[:] = mix[b][:, lo:lo + half + 128]
            xTt[:] = x[b, lo:lo + half + 128].T
        mapsB.append(dict(mixT=mT, xT=xTt, w_out0=e_w_out[0], w_in1=o_w_in[0], w_out1=o_w_out[0], vecs=vecs, cw1=cw1, cbdB=np.ascontiguousarray(cb[:, 0:128])))
    resB = run_bass_kernel_spmd(ncB, mapsB, core_ids=list(range(8)))
    out = np.empty((Bn, S, Dm), np.float32)
    for c in range(8):
        b, p = c // 2, c % 2
        out[b, half * p:half * (p + 1), :] = np.asarray(resB.results[c]["yT"]).T
    return out


def kernel(x, e_norm_pre, e_w_in, e_conv_w, e_conv_b, e_dt_bias, e_a_log, e_d_skip, e_fgate_b,
           e_ssd_norm, e_w_out, e_norm_post, o_norm_pre, o_w_in, o_conv_w, o_conv_b, o_ln_g, o_ln_b,
           o_w_out, o_norm_post):
    args = [np.asarray(a, np.float32) for a in (x, e_norm_pre, e_w_in, e_conv_w, e_conv_b, e_dt_bias, e_a_log, e_d_skip, e_fgate_b,
                                                 e_ssd_norm, e_w_out, e_norm_post, o_norm_pre, o_w_in, o_conv_w, o_conv_b, o_ln_g, o_ln_b,
                                                 o_w_out, o_norm_post)]
    (x, e_norm_pre, e_w_in, e_conv_w, e_conv_b, e_dt_bias, e_a_log, e_d_skip, e_fgate_b,
     e_ssd_norm, e_w_out, e_norm_post, o_norm_pre, o_w_in, o_conv_w, o_conv_b, o_ln_g, o_ln_b, o_w_out, o_norm_post) = args
    Bn, S, Dm = x.shape
    half = S // 2
    cf, cb = _consts_np()
    key = ('F', S)
    if key not in _PROGS:
        nc = bass.Bass("TRN2", target_bir_lowering=False)
        build_fused(nc, S)
        _PROGS[key] = nc
    nc = _PROGS[key]
    vecs = np.zeros((128, 88), np.float32)
    vecs[:, 0:8] = _pk(e_norm_post[0], 8)
    vecs[:, 8:16] = _pk(o_norm_pre[0], 8)
    vecs[:, 16:24] = _pk(o_norm_post[0], 8)
    vecs[:, 24:40] = _pk(o_conv_b[0], 16)
    vecs[:, 40:56] = _pk(o_ln_g[0], 16)
    vecs[:, 56:72] = _pk(o_ln_b[0], 16)
    cw1 = np.ascontiguousarray(o_conv_w[0].reshape(31, 16, 128).transpose(2, 1, 0))
    wo = e_w_out[0]
    rows = []
    for k in range(8):
        for r in range(2):
            base = (512 * r + 128 * k) if k < 4 else (1024 + 512 * r + 128 * (k - 4))
            rows.append(wo[base:base + 128])
    w_out0 = np.ascontiguousarray(np.concatenate(rows, axis=0))
    dgf = np.zeros((16, 128, 31, 128), np.float32)
    ii = np.arange(128)
    for j in range(16):
        dgf[j, ii, :, ii] = o_conv_w[0][:, j * 128:(j + 1) * 128].T
    dgf = dgf.reshape(16, 128, 31 * 128)
    maps = []
    for c in range(8):
        b, p = c // 2, c % 2
        m = _inputs_A(b, p, x, e_norm_pre, e_w_in, e_conv_w, e_conv_b, e_dt_bias, e_a_log, e_d_skip, e_fgate_b, e_ssd_norm, cf, cb)
        xTt = np.zeros((1024, half + 128), np.float32)
        lo = half * p - 128
        if lo < 0:
            xTt[:, 128:] = x[b, 0:half].T
        else:
            xTt[:] = x[b, lo:lo + half + 128].T
        msel = np.zeros((128, 2), np.float32)
        msel[:, p] = 1.0
        m.update(xTB=xTt, msel=msel, w_out0=w_out0, w_in1=o_w_in[0], w_out1=o_w_out[0], vecs=vecs, cw1=cw1, cbdB=np.ascontiguousarray(cb[:, 0:128]), dgf=dgf)
        maps.append(m)
    res = run_bass_kernel_spmd(nc, maps, core_ids=list(range(8)))
    out = np.empty((Bn, S, Dm), np.float32)
    for c in range(8):
        b, p = c // 2, c % 2
        out[b, half * p:half * (p + 1), :] = np.asarray(res.results[c]["yT"]).T
    return out
```

```python
import numpy as np
from contextlib import ExitStack
import ml_dtypes
import concourse.bass as bass
import concourse.mybir as mybir
from concourse.bass_utils import run_bass_kernel_spmd

AF = mybir.ActivationFunctionType
ALU = mybir.AluOpType
F32 = mybir.dt.float32
BF16 = mybir.dt.bfloat16
EPS = 1e-6


class Prog:
    NDMASEM = 16

    def __init__(self, nc):
        self.nc = nc
        self.ops = []
        self.last_w = {}
        self.readers = {}
        self.bar = None
        self.bar_done = set()
        self.last_on = {}
        self.dma_recent = {}

    def barrier(self):
        b = list(self.last_on.values())
        for q in self.dma_recent.values():
            b.extend(q)
        self.bar = b
        self.bar_done = set()

    BANK = {'pb0': ('B0',), ('pj', 0): ('B0',), ('pj', 1): ('B1',), ('pj', 2): ('B2',), ('pj', 3): ('B3',),
            'pd': ('B2',), 'pc': ('B2',), 'pt8': ('B2',), 'pr': ('B3',),
            'pcb': ('B4',), 'pst': ('B4',), 'py': ('B5',), 'pmean': ('B5',),
            'pn': ('B6',), 'pmsq': ('B6',), 'ptb': ('B7',), ('pcv', 0): ('B3',), ('pcv', 1): ('B7',),
            ('st', 0): ('B0', 'B1'), ('st', 1): ('B2', 'B3'), ('st', 2): ('B4', 'B5'), ('ob', 0): ('B6',), ('ob', 1): ('B7',)}

    def op(self, eng, fn, r=(), w=(), dma=False, cc=False):
        idx = len(self.ops)
        deps = {}
        banks = set()
        for k in r:
            if k in self.BANK:
                banks.update(self.BANK[k])
        for k in w:
            if k in self.BANK:
                banks.update(self.BANK[k])
        if banks:
            w = list(w) + sorted(banks)
        for k in r:
            lw = self.last_w.get(k)
            if lw is not None:
                deps[lw] = True
        for k in w:
            lw = self.last_w.get(k)
            if lw is not None:
                deps[lw] = True
            rd = self.readers.get(k)
            if rd:
                for i in rd[0].values():
                    if i not in deps:
                        deps[i] = False
                for i in rd[1]:
                    if i not in deps:
                        deps[i] = False
        if self.bar is not None and eng not in self.bar_done:
            for i in self.bar:
                deps[i] = True
            self.bar_done.add(eng)
        for k in r:
            rd = self.readers.setdefault(k, ({}, []))
            if dma:
                rd[1].append(idx)
            else:
                rd[0][eng] = idx
        for k in w:
            self.last_w[k] = idx
            self.readers[k] = ({}, [])
        self.ops.append(dict(eng=eng, fn=fn, deps=deps, dma=dma, cc=cc))
        if dma and not cc:
            q = self.dma_recent.setdefault(eng, [])
            q.append(idx)
            if len(q) > self.NDMASEM:
                q.pop(0)
        else:
            self.last_on[eng] = idx
        return idx

    def _skip(self, od, o):
        return (not od['dma']) and (not o['dma']) and od['eng'] == o['eng']

    def finalize(self, sems):
        ops = self.ops
        n = len(ops)
        need = [False] * n
        for o in ops:
            for i, hard in o['deps'].items():
                oi = ops[i]
                if oi['dma']:
                    continue
                if self._skip(oi, o) and (oi['eng'] == 'pe' or not hard):
                    continue
                need[i] = True
        cnt = {}
        dcnt = {}
        sig = [None] * n
        P = self.NDMASEM
        for i, o in enumerate(ops):
            e = o['eng']
            if o['cc']:
                k = dcnt.get('cc', 0)
                dcnt['cc'] = k + 1
                sig[i] = (sems['cc'][k], 1)
            elif o['dma']:
                k = dcnt.get(e, 0)
                dcnt[e] = k + 1
                sig[i] = (sems['dma_' + e][k % P], 16 * (k // P + 1))
            elif need[i]:
                cnt[e] = cnt.get(e, 0) + 1
                sig[i] = (sems[e], cnt[e])
        self.sig = sig
        engs = {}
        for i, o in enumerate(ops):
            engs.setdefault(o['eng'], []).append(i)
        self.engs = engs

    def final_values(self):
        last = {}
        for i, o in enumerate(self.ops):
            if self.sig[i] is not None:
                s = self.sig[i]
                prev = last.get(id(s[0]))
                if prev is None or prev[1] < s[1]:
                    last[id(s[0])] = (s[0], s[1])
        return list(last.values())

    def emit_engine(self, e, h, final=False):
        ops = self.ops
        sig = self.sig
        waited = {}
        for sem, val in self.pre_waits:
            h.wait_ge(sem, val)

        def wait(sem, val):
            key = id(sem)
            if waited.get(key, 0) >= val:
                return
            waited[key] = val
            h.wait_ge(sem, val)

        for i in self.engs.get(e, []):
            o = ops[i]
            for d, hard in o['deps'].items():
                od = ops[d]
                if self._skip(od, o) and (e == 'pe' or not hard):
                    continue
                s = sig[d]
                wait(s[0], s[1])
            if o['cc']:
                o['fn'](h).then_inc(sig[i][0])
            elif o['dma']:
                sem, val = sig[i]
                if val > 16:
                    wait(sem, val - 16)
                o['fn'](h).then_inc(sem, 16)
            else:
                ins = o['fn'](h)
                if sig[i] is not None:
                    ins.then_inc(sig[i][0], 1)
        if final:
            last = {}
            for i, o in enumerate(ops):
                if sig[i] is not None:
                    s = sig[i]
                    prev = last.get(id(s[0]))
                    if prev is None or prev[1] < s[1]:
                        last[id(s[0])] = (s[0], s[1])
            for sem, val in last.values():
                wait(sem, val)

    @classmethod
    def make_sems(cls, nc, es, tag=""):
        sems = {}
        for e in ['pe', 'act', 'dve', 'pool']:
            sems[e] = es.enter_context(nc.semaphore(tag + "s_" + e))
        sems['dma_sp'] = [es.enter_context(nc.semaphore(tag + "dq%d" % i)) for i in range(cls.NDMASEM)]
        sems['dma_pool'] = [es.enter_context(nc.semaphore(tag + "dp%d" % i)) for i in range(cls.NDMASEM)]
        sems['cc'] = [es.enter_context(nc.semaphore(tag + "ccs%d" % i)) for i in range(8)]
        return sems

    def run(self, es, sems=None, pre_waits=()):
        nc = self.nc
        if sems is None:
            sems = self.make_sems(nc, es)
        self.pre_waits = list(pre_waits)
        self.finalize(sems)
        p = self
        with nc.Block() as block:
            @block.tensor
            def _(e):
                p.emit_engine('pe', e)

            @block.scalar
            def _(e):
                p.emit_engine('act', e)

            @block.vector
            def _(e):
                p.emit_engine('dve', e)

            @block.gpsimd
            def _(e):
                p.emit_engine('pool', e)

            @block.sync
            def _(e):
                p.emit_engine('sp', e, final=True)


class Arena:
    def __init__(self, t, nwords):
        self.t = t
        self.n = nwords
        self.off = 0

    def reset(self):
        self.off = 0

    def _shape(self, v, shape):
        if len(shape) == 2:
            return v
        if len(shape) == 3:
            return v.rearrange("p (a b) -> p a b", a=shape[1])
        raise ValueError(shape)

    def f32(self, shape):
        n = int(np.prod(shape[1:]))
        v = self.t[0:shape[0], self.off:self.off + n]
        self.off += n
        assert self.off <= self.n, (self.off, self.n)
        return self._shape(v, shape)

    def bf16(self, shape):
        n = int(np.prod(shape[1:]))
        w = (n + 1) // 2
        v = self.t[0:shape[0], self.off:self.off + w].bitcast(BF16)
        self.off += w
        assert self.off <= self.n, (self.off, self.n)
        if 2 * w != n:
            v = v[:, 0:n]
        return self._shape(v, shape)


def _consts_np():
    U = np.triu(np.ones((128, 128), np.float32))
    ident = np.eye(128, dtype=np.float32)
    swap = np.zeros((128, 128), np.float32)
    for k in range(128):
        swap[k, (k + 64) % 128] = 1.0
    mj = np.zeros((128, 4, 512), np.float32)
    kk = np.arange(128)[:, None]
    qq = np.arange(512)[None, :]
    for j in range(4):
        mj[:, j, :] = np.where(qq < 128 * j + kk, -30000.0, 0.0)
    cf = np.concatenate([U, ident, swap], axis=1)
    cb = np.concatenate([ident, mj.reshape(128, 2048)], axis=1).astype(ml_dtypes.bfloat16)
    return cf, cb


def build_A(nc, T, stages=('s0', 'ssd', 'f0', 'fox'), fused=None):
    NB = T // 512
    NCH = T // 128
    D = lambda name, shape, dt, kind=None: (nc.dram_tensor(name, shape, dt, kind=kind).ap() if kind else nc.dram_tensor(name, shape, dt).ap())
    xT = D("xT", [1024, T], F32, "ExternalInput")
    gpre = D("gpre", [128, 8], F32, "ExternalInput")
    wssd = D("wssd", [2, 1024, 768], F32, "ExternalInput")
    wdt = D("wdt", [1024, 8], F32, "ExternalInput")
    cw = D("cw", [128, 8, 4], F32, "ExternalInput")
    cbv = D("cbv", [128, 8], F32, "ExternalInput")
    dtb16 = D("dtb16", [128, 2, 16], F32, "ExternalInput")
    alog16 = D("alog16", [128, 2, 16], F32, "ExternalInput")
    dsk = D("dsk", [128, 4], F32, "ExternalInput")
    ssdn = D("ssdn", [128, 4], F32, "ExternalInput")
    wfox = D("wfox", [8, 1024, 192], F32, "ExternalInput")
    wzf = D("wzf", [1024, 512], F32, "ExternalInput")
    wf = D("wf", [1024, 8], F32, "ExternalInput")
    fb = D("fb", [8, 1], F32, "ExternalInput")
    cfd = D("cfd", [128, 384], F32, "ExternalInput")
    cbd = D("cbd", [128, 2176], BF16, "ExternalInput")
    cneg = D("cneg", [3, 512], BF16, "ExternalInput")
    mixT = None if fused else D("mixT", [1024, T], BF16, "ExternalOutput")

    def mix_dst(r0, n, blk):
        if fused:
            return fused["mixloc_bf"][r0 // 128][r0 % 128:r0 % 128 + n, blk]
        return mixT[r0:r0 + n, blk]
    crow = D("crow", [3, 8, T], BF16)

    with ExitStack() as es:
        sb = lambda n, s, d: es.enter_context(nc.sbuf_tensor(n, s, d))
        ps = lambda n, s, d: es.enter_context(nc.psum_tensor(n, s, d))
        uT = sb("uT", [128, 8, T], BF16)
        cf = sb("cf", [128, 384], F32)
        cb = sb("cb", [128, 2176], BF16)
        U = cf[:, 0:128]
        identf = cf[:, 128:256]
        swapm = cf[:, 256:384]
        identb = cb[:, 0:128]
        maskJ = cb[:, 128:2176].rearrange("p (j q) -> p j q", j=4)
        sm = sb("sm", [128, 160], F32)
        gpre_s = sm[:, 0:8]
        cbv_s = sm[:, 8:16]
        dsk_s = sm[:, 16:20]
        ssdn_s = sm[:, 20:24]
        epsb = sm[:, 24:25]
        oneb = sm[:, 25:26]
        nfb = sm[:, 26:27]
        fb_s = sm[:, 27:28]
        dtb_s = sm[:, 32:64].rearrange("p (g c) -> p g c", g=2)
        a16 = sm[:, 64:96].rearrange("p (g c) -> p g c", g=2)
        cw_s = sm[:, 96:128].rearrange("p (t k) -> p t k", t=8)
        onesb = sb("onesb", [128, 256], BF16)
        onesf = sb("onesf", [128, 128], F32)
        ncol = sb("ncol", [128, NCH, 8], F32)
        AW = 17408
        arena_t = sb("arena", [128, AW], F32)
        ar = Arena(arena_t, AW)
        PP = [ps("pp%d" % i, [128, 1024], F32) for i in range(4)]
        PB = [PP[i // 2][:, (i % 2) * 512:(i % 2) * 512 + 512] for i in range(8)]
        PTB = PB[7].bitcast(BF16)

        p = Prog(nc)
        mo_keys = [[] for _ in range(8)]
        dma = lambda out, in_, r, w: p.op('sp', lambda e: e.dma_start(out=out, in_=in_), r=r, w=w, dma=True)

        if fused:
            cast_q = list(fused['cast_pieces'])
        else:
            cast_q = []

        def emit_casts(n):
            for _ in range(n):
                if cast_q:
                    src, dst = cast_q.pop(0)
                    p.op('pool', lambda e, src=src, dst=dst: e.dma_start(out=dst, in_=src), r=[], w=[], dma=True)
        if True:
            pass
        dma(cf[:], cfd, [], ['cf'])
        dma(cb[:], cbd, [], ['cb'])
        dma(gpre_s, gpre, [], ['sm'])
        dma(cbv_s, cbv, [], ['sm'])
        dma(dsk_s, dsk, [], ['sm'])
        dma(ssdn_s, ssdn, [], ['sm'])
        dma(fb_s[0:8, :], fb, [], ['sm'])
        dma(dtb_s, dtb16, [], ['sm'])
        dma(a16, alog16, [], ['sm'])
        dma(cw_s, cw, [], ['sm'])
        p.op('pool', lambda e: e.memset(epsb, EPS), w=['c1'])
        p.op('pool', lambda e: e.memset(oneb, 1.0), w=['c1'])
        p.op('pool', lambda e: e.memset(onesb[:, 0:128], 1.0 / 1024), w=['c1'])
        p.op('pool', lambda e: e.memset(onesb[:, 128:256], 1.0 / 256), w=['c1'])
        p.op('pool', lambda e: e.memset(onesf[:], 1.0), w=['c1'])
        p.op('dve', lambda e: e.tensor_scalar(out=nfb[0:8, :], in0=fb_s[0:8, :], scalar1=-1.0, scalar2=None, op0=ALU.mult), r=['sm'], w=['nfb'])
        p.op('act', lambda e: e.activation(out=a16, in_=a16, func=AF.Exp), r=['sm'], w=['a16'])
        p.op('dve', lambda e: e.tensor_scalar(out=a16, in0=a16, scalar1=-1.0, scalar2=None, op0=ALU.mult), r=['a16'], w=['a16'])
        CK = ['cf', 'cb', 'sm', 'c1', 'nfb', 'a16']

        if 's0' in stages:
            ar.reset()
            XB = [ar.f32([128, 8, 512]) for _ in range(2)]
            SQ = [ar.bf16([128, 8, 512]) for _ in range(2)]
            TMP = [ar.f32([128, 512]) for _ in range(2)]
            RS = [ar.f32([128, 512]) for _ in range(2)]
            for tb in range(NB):
                i = tb % 2
                xb, sq, tmp, rs = XB[i], SQ[i], TMP[i], RS[i]
                blk = slice(tb * 512, (tb + 1) * 512)
                dma(xb, xT[:, blk].rearrange("(k p) t -> p k t", p=128), [], [('xb', i)])
                p.op('pool', lambda e, xb=xb, sq=sq: e.tensor_tensor(out=sq, in0=xb, in1=xb, op=ALU.mult), r=[('xb', i)], w=[('sq', i)])
                for k in range(8):
                    p.op('pe', lambda e, k=k, sq=sq: e.matmul(PB[0][:, :], lhsT=onesb[:, 0:128], rhs=sq[:, k, :], start=(k == 0), stop=(k == 7)),
                         r=[('sq', i), 'c1'], w=['pb0'])
                p.op('act', lambda e, tmp=tmp: e.activation(out=tmp, in_=PB[0][:, :], func=AF.Sqrt, bias=epsb), r=['pb0', 'c1'], w=[('tmp', i)])
                p.op('dve', lambda e, tmp=tmp, rs=rs: e.reciprocal(out=rs, in_=tmp), r=[('tmp', i)], w=[('rs', i)])
                for k in range(8):
                    p.op('dve', lambda e, k=k, xb=xb, rs=rs, blk=blk: e.scalar_tensor_tensor(
                        out=uT[:, k, blk], in0=xb[:, k, :], scalar=gpre_s[:, k:k + 1], in1=rs, op0=ALU.mult, op1=ALU.mult),
                        r=[('xb', i), ('rs', i), 'sm'], w=[('uT', tb)])
            p.barrier()

        if 'ssd' in stages:
            ar.reset()
            wst = [ar.f32([128, 8, 128])] * 2
            wg = ar.bf16([128, 8, 768])
            wdt_f = ar.f32([128, 8, 8])
            wdt_b = ar.bf16([128, 8, 8])
            XC = ar.f32([128, 4, 515])
            zs = ar.bf16([128, 2, 512])
            ACC = [ar.f32([128, 512]) for _ in range(2)]
            ft = ar.bf16([128, 4, 512])
            xs_tok2 = [ar.bf16([128, 256]) for _ in range(2)]
            B_tok2 = [ar.bf16([128, 128]) for _ in range(2)]
            DTB = [dict(dtr=ar.f32([128, 16]), dte1=ar.f32([128, 16]), dtv=ar.f32([128, 16]), dav=ar.f32([128, 16]),
                        csv=ar.f32([128, 16]), ncs=ar.f32([128, 16])) for _ in range(2)]
            clv2 = [ar.f32([128, 4]) for _ in range(2)]
            dtmp = ar.f32([128, 4])
            dtev = ar.f32([128, 4])
            cdv = ar.f32([128, 4])
            UD2 = [ar.f32([128, 4, 128]) for _ in range(2)]
            Ev2 = [ar.f32([128, 4, 128]) for _ in range(2)]
            E2v2 = [ar.f32([128, 4, 128]) for _ in range(2)]
            MT2 = [ar.bf16([128, 4, 128]) for _ in range(2)]
            CsT2 = [ar.bf16([128, 4, 128]) for _ in range(2)]
            CBm2 = [ar.f32([128, 128]) for _ in range(2)]
            xdp2 = [ar.bf16([128, 4, 128]) for _ in range(2)]
            xdd = ar.bf16([128, 256])
            hT = ar.f32([128, 256])
            hTp = ar.bf16([128, 4, 128])
            Y = ar.f32([128, 2, 512])
            sqy = ar.bf16([128, 2, 512])
            tmpn = ACC[0]
            rsn = ACC[1]
            ob = ar.bf16([128, 2, 512])
            PJ = [PB[0], PB[1]]
            PM = PB[2]
            PR = PB[3]
            PCB = PB[4]
            PY = PB[5]
            PN = PB[6]
            dma(wdt_f, wdt.rearrange("(k p) c -> p k c", p=128), [], ['wdt_f'])
            p.op('pool', lambda e: e.tensor_copy(out=wdt_b, in_=wdt_f), r=['wdt_f'], w=['wdt_b'])
            pj_i = 0
            ch_i = 0
            dt_i = 0

            def dt_stage(g, tb, di):
                d = DTB[di]
                dtr, dte1, dtv, dav, csv, ncs = d['dtr'], d['dte1'], d['dtv'], d['dav'], d['csv'], d['ncs']
                K = lambda n: (n, di)
                for c in range(4):
                    tok = slice(tb * 512 + c * 128, tb * 512 + (c + 1) * 128)
                    for k in range(8):
                        p.op('pe', lambda e, c=c, k=k, tok=tok: e.matmul(PM[:, c * 4:(c + 1) * 4], lhsT=uT[:, k, tok], rhs=wdt_b[:, k, g * 4:(g + 1) * 4],
                                                                       start=(k == 0), stop=(k == 7)), r=[('uT', tb), 'wdt_b'], w=['pd'])
                p.op('dve', lambda e: e.tensor_tensor(out=dtr, in0=PM[:, 0:16], in1=dtb_s[:, g, :], op=ALU.add), r=['pd', 'sm'], w=[K('dtr')])
                p.op('act', lambda e: e.activation(out=dte1, in_=dtr, func=AF.Exp), r=[K('dtr')], w=[K('dte1')])
                p.op('act', lambda e: e.activation(out=dtv, in_=dte1, func=AF.Ln, bias=oneb), r=[K('dte1'), 'c1'], w=[K('dtv')])
                p.op('dve', lambda e: e.tensor_tensor(out=dav, in0=dtv, in1=a16[:, g, :], op=ALU.mult), r=[K('dtv'), 'a16'], w=[K('dav')])
                p.op('pe', lambda e: e.matmul(PM[:, 16:32], lhsT=U, rhs=dav, start=True, stop=True), r=['cf', K('dav')], w=['pc'])
                p.op('dve', lambda e: e.tensor_copy(out=csv, in_=PM[:, 16:32]), r=['pc'], w=[K('csv')])
                p.op('dve', lambda e: e.tensor_scalar(out=ncs, in0=csv, scalar1=-1.0, scalar2=None, op0=ALU.mult), r=[K('csv')], w=[K('ncs')])
            for g in range(2):
                for j in range(6):
                    st = wst[j % 2]
                    dma(st, wssd[g, :, j * 128:(j + 1) * 128].rearrange("(k p) c -> p k c", p=128), [], [('wst', 0)])
                    p.op('pool', lambda e, st=st, j=j: e.tensor_copy(out=wg[:, :, j * 128:(j + 1) * 128], in_=st), r=[('wst', 0)], w=['wg'])
                p.op('pool', lambda e: e.memset(XC[:, :, 0:3], 0.0), w=[('XC', j) for j in range(4)])
                for i_ in range(2):
                    p.op('pool', lambda e, i_=i_: e.memset(xdp2[i_], 0.0), w=[('xdp', i_)])
                p.op('pool', lambda e: e.memset(hTp, 0.0), w=['hTp'])
                p.op('pool', lambda e: e.memset(hT, 0.0), w=['hT'])
                dt_stage(g, 0, dt_i % 2)
                for tb in range(NB):
                    blk = slice(tb * 512, (tb + 1) * 512)
                    emit_casts(3 if NB >= 16 else 56)
                    for j in range(6):
                        pj = PJ[pj_i % 2]
                        pk = ('pj', pj_i % 2)
                        pj_i += 1
                        for k in range(8):
                            p.op('pe', lambda e, pj=pj, j=j, k=k, blk=blk: e.matmul(pj[:, :], lhsT=wg[:, k, j * 128:(j + 1) * 128], rhs=uT[:, k, blk],
                                                                                 start=(k == 0), stop=(k == 7)), r=['wg', ('uT', tb)], w=[pk])
                        if j < 2:
                            p.op('act', lambda e, pj=pj, j=j: e.activation(out=zs[:, j, :], in_=pj[:, :], func=AF.Silu), r=[pk], w=[('zs', j)])
                        else:
                            p.op('act', lambda e, pj=pj, j=j: e.activation(out=XC[:, j - 2, 3:515], in_=pj[:, :], func=AF.Copy), r=[pk], w=[('XC', j - 2)])
                    for j in range(4):
                        eng = 'dve'
                        acc = ACC[j % 2]
                        t = g * 4 + j
                        p.op(eng, lambda e, acc=acc, j=j, t=t: e.tensor_scalar(out=acc, in0=XC[:, j, 3:515], scalar1=cw_s[:, t, 3:4], scalar2=cbv_s[:, t:t + 1],
                                                                          op0=ALU.mult, op1=ALU.add), r=[('XC', j), 'sm'], w=[('acc', j % 2)])
                        for s in range(3):
                            p.op(eng, lambda e, acc=acc, j=j, t=t, s=s: e.scalar_tensor_tensor(out=acc, in0=XC[:, j, s:s + 512], scalar=cw_s[:, t, s:s + 1], in1=acc,
                                                                                           op0=ALU.mult, op1=ALU.add), r=[('XC', j), ('acc', j % 2)], w=[('acc', j % 2)])
                        p.op('act', lambda e, acc=acc, j=j: e.activation(out=ft[:, j, :], in_=acc, func=AF.Silu), r=[('acc', j % 2)], w=[('ft', j)])
                        p.op(eng, lambda e, j=j: e.tensor_copy(out=XC[:, j, 0:3], in_=XC[:, j, 512:515]), r=[('XC', j)], w=[('XC', j)])
                    di = dt_i % 2
                    dt_i += 1
                    if tb + 1 < NB:
                        dt_stage(g, tb + 1, dt_i % 2)
                    dtv, dav, csv, ncs = DTB[di]['dtv'], DTB[di]['dav'], DTB[di]['csv'], DTB[di]['ncs']
                    KD = lambda n, di=di: (n, di)
                    def front(c, ci, tb=tb, g=g, dtv=dtv, dav=dav, ncs=ncs, KD=KD):
                        ck = slice(c * 128, (c + 1) * 128)
                        xs_tok, B_tok, UD, Ev, E2v, MT, CsT, CBm, xdp, clv = xs_tok2[ci], B_tok2[ci], UD2[ci], Ev2[ci], E2v2[ci], MT2[ci], CsT2[ci], CBm2[ci], xdp2[ci], clv2[ci]
                        K = lambda n: (n, ci)
                        for j in range(3):
                            src = ft[:, j if j < 2 else 2, ck]
                            p.op('pe', lambda e, j=j, src=src: e.transpose(out=PTB[:, j * 128:(j + 1) * 128], in_=src, identity=identb),
                                 r=[('ft', j if j < 2 else 2), 'cb'], w=['ptb'])
                        p.op('dve', lambda e: e.tensor_copy(out=xs_tok, in_=PTB[:, 0:256]), r=['ptb'], w=[K('xs_tok')])
                        p.op('dve', lambda e: e.tensor_copy(out=B_tok, in_=PTB[:, 256:384]), r=['ptb'], w=[K('B_tok')])
                        for h in range(4):
                            hs = slice((h % 2) * 64, (h % 2) * 64 + 64)
                            p.op('dve', lambda e, h=h, hs=hs: e.tensor_scalar(out=xdp[:, h, hs], in0=xs_tok[:, h * 64:(h + 1) * 64],
                                                                          scalar1=dtv[:, c * 4 + h:c * 4 + h + 1], scalar2=None, op0=ALU.mult),
                                 r=[K('xs_tok'), KD('dtv')], w=[K('xdp')])
                        for h in range(4):
                            p.op('pe', lambda e, h=h: e.matmul(PR[:, h * 128:(h + 1) * 128], lhsT=dav[:, c * 4 + h:c * 4 + h + 1].to_broadcast([128, 128]), rhs=U, start=True, stop=True),
                                 r=['cf', KD('dav')], w=['pr'])
                        p.op('pe', lambda e: e.matmul(PCB[:, 0:128], lhsT=ft[:, 2, ck], rhs=ft[:, 3, ck], start=True, stop=True),
                             r=[('ft', 2), ('ft', 3)], w=['pcb'])
                        p.op('dve', lambda e: e.tensor_tensor(out=CBm, in0=PCB[:, 0:128], in1=U, op=ALU.mult), r=['pcb', 'cf'], w=[K('CBm')])
                        for h in range(4):
                            p.op('dve', lambda e, h=h: e.tensor_scalar(out=Ev[:, h, :], in0=PR[:, h * 128:(h + 1) * 128], scalar1=ncs[:, c * 4 + h:c * 4 + h + 1],
                                                                    scalar2=0.0, op0=ALU.add, op1=ALU.min), r=['pr', KD('ncs')], w=[K('Ev')])
                        p.op('act', lambda e: e.activation(out=E2v, in_=PR[:, :].rearrange("p (h l) -> p h l", h=4), func=AF.Exp), r=['pr'], w=[K('E2v')])
                        p.op('dve', lambda e: e.tensor_copy(out=clv, in_=PR[:, :].rearrange("p (h l) -> p h l", h=4)[:, :, 127]), r=['pr'], w=[K('clv')])
                        p.op('act', lambda e: e.activation(out=Ev, in_=Ev, func=AF.Exp), r=[K('Ev')], w=[K('Ev')])
                        p.op('pool', lambda e: e.tensor_tensor(out=MT, in0=Ev, in1=CBm.unsqueeze(1).to_broadcast([128, 4, 128]), op=ALU.mult), r=[K('Ev'), K('CBm')], w=[K('MT')])
                        p.op('pool', lambda e: e.tensor_tensor(out=CsT, in0=E2v, in1=ft[:, 3, ck].unsqueeze(1).to_broadcast([128, 4, 128]), op=ALU.mult),
                             r=[K('E2v'), ('ft', 3)], w=[K('CsT')])

                    def back(c, ci, tb=tb, g=g, csv=csv, KD=KD):
                        ck = slice(c * 128, (c + 1) * 128)
                        B_tok, MT, CsT, xdp, clv = B_tok2[ci], MT2[ci], CsT2[ci], xdp2[ci], clv2[ci]
                        K = lambda n: (n, ci)
                        for pr in range(2):
                            n = 0
                            for h in (2 * pr, 2 * pr + 1):
                                p.op('pe', lambda e, h=h, pr=pr, n=n: e.matmul(PY[:, pr * 128:(pr + 1) * 128], lhsT=xdp[:, h, :], rhs=MT[:, h, :], start=(n == 0), stop=False),
                                     r=[K('xdp'), K('MT')], w=['py'])
                                n += 1
                                p.op('pe', lambda e, h=h, pr=pr, n=n: e.matmul(PY[:, pr * 128:(pr + 1) * 128], lhsT=hTp[:, h, :], rhs=CsT[:, h, :], start=False, stop=(n == 3)),
                                     r=['hTp', K('CsT')], w=['py'])
                                n += 1
                            p.op('dve', lambda e, pr=pr: e.scalar_tensor_tensor(out=Y[:, pr, ck], in0=ft[:, pr, ck], scalar=dsk_s[:, g * 2 + pr:g * 2 + pr + 1],
                                                                            in1=PY[:, pr * 128:(pr + 1) * 128], op0=ALU.mult, op1=ALU.add),
                                 r=[('ft', pr), 'py', 'sm'], w=[('Y', pr)])
                        p.op('dve', lambda e: e.tensor_tensor(out=dtmp, in0=clv, in1=csv[:, c * 4:(c + 1) * 4], op=ALU.subtract), r=[K('clv'), KD('csv')], w=['dtmp'])
                        p.op('act', lambda e: e.activation(out=dtev, in_=dtmp, func=AF.Exp), r=['dtmp'], w=['dtev'])
                        p.op('act', lambda e: e.activation(out=cdv, in_=clv, func=AF.Exp), r=[K('clv')], w=['cdv'])
                        for h in range(4):
                            hs = slice((h % 2) * 64, (h % 2) * 64 + 64)
                            p.op('dve', lambda e, h=h, hs=hs: e.tensor_scalar(out=xdd[:, h * 64:(h + 1) * 64], in0=xdp[:, h, hs], scalar1=dtev[:, h:h + 1], scalar2=None, op0=ALU.mult),
                                 r=[K('xdp'), 'dtev'], w=['xdd'])
                        p.op('pe', lambda e: e.matmul(PCB[:, 128:384], lhsT=B_tok, rhs=xdd, start=True, stop=True), r=[K('B_tok'), 'xdd'], w=['pst'])
                        for h in range(4):
                            hs = slice((h % 2) * 64, (h % 2) * 64 + 64)
                            p.op('dve', lambda e, h=h: e.scalar_tensor_tensor(out=hT[:, h * 64:(h + 1) * 64], in0=hT[:, h * 64:(h + 1) * 64], scalar=cdv[:, h:h + 1],
                                                                          in1=PCB[:, 128 + h * 64:128 + (h + 1) * 64], op0=ALU.mult, op1=ALU.add),
                                 r=['hT', 'cdv', 'pst'], w=['hT'])
                            p.op('pool', lambda e, h=h, hs=hs: e.tensor_copy(out=hTp[:, h, hs], in_=hT[:, h * 64:(h + 1) * 64]), r=['hT'], w=['hTp'])

                    front(0, ch_i % 2)
                    for c in range(4):
                        ci = ch_i % 2
                        ch_i += 1
                        if c + 1 < 4:
                            front(c + 1, ch_i % 2)
                        back(c, ci)
                    for pr in range(2):
                        p.op('pool', lambda e, pr=pr: e.tensor_tensor(out=Y[:, pr, :], in0=Y[:, pr, :], in1=zs[:, pr, :], op=ALU.mult), r=[('Y', pr), ('zs', pr)], w=[('Y', pr)])
                        p.op('pool', lambda e, pr=pr: e.tensor_tensor(out=sqy[:, pr, :], in0=Y[:, pr, :], in1=Y[:, pr, :], op=ALU.mult), r=[('Y', pr)], w=[('sqy', pr)])
                    for pr in range(2):
                        p.op('pe', lambda e, pr=pr: e.matmul(PN[:, :], lhsT=onesb[:, 128:256], rhs=sqy[:, pr, :], start=(pr == 0), stop=(pr == 1)),
                             r=[('sqy', pr), 'c1'], w=['pn'])
                    p.op('act', lambda e: e.activation(out=tmpn, in_=PN[:, :], func=AF.Sqrt, bias=epsb), r=['pn', 'c1'], w=[('acc', 0)])
                    p.op('dve', lambda e: e.reciprocal(out=rsn, in_=tmpn), r=[('acc', 0)], w=[('acc', 1)])
                    for pr in range(2):
                        p.op('dve', lambda e, pr=pr, g=g: e.scalar_tensor_tensor(out=ob[:, pr, :], in0=Y[:, pr, :], scalar=ssdn_s[:, g * 2 + pr:g * 2 + pr + 1], in1=rsn,
                                                                        op0=ALU.mult, op1=ALU.mult), r=[('Y', pr), ('acc', 1), 'sm'], w=[('ob', pr)])
                        r0 = g * 256 + pr * 128
                        dma(mix_dst(r0, 128, blk), ob[:, pr, :], [('ob', pr)], [('mo', len(p.ops))])
                        mo_keys[r0 // 128].append(('mo', len(p.ops) - 1))
            p.barrier()

        emit_casts(1000)
        if 'f0' in stages:
            ar.reset()
            wf_f = ar.f32([128, 8, 8])
            wf_b = ar.bf16([128, 8, 8])
            ones8 = ar.f32([128, 512])
            carry = ar.f32([128, 1])
            EV = [ar.f32([128, 512]) for _ in range(2)]
            SP = [ar.f32([128, 512]) for _ in range(2)]
            CSB = [ar.f32([128, 512]) for _ in range(2)]
            R1 = [ar.f32([128, 512]) for _ in range(2)]
            R2 = [ar.f32([128, 512]) for _ in range(2)]
            HI = [ar.bf16([128, 512]) for _ in range(2)]
            MID = [ar.bf16([128, 512]) for _ in range(2)]
            LO = [ar.bf16([128, 512]) for _ in range(2)]
            dma(wf_f, wf.rearrange("(k p) c -> p k c", p=128), [], ['wf_f'])
            p.op('pool', lambda e: e.tensor_copy(out=wf_b, in_=wf_f), r=['wf_f'], w=['wf_b'])
            p.op('pool', lambda e: e.memset(ones8, 1.0), w=['ones8'])
            for tb in range(NB):
                i = tb % 2
                blk = slice(tb * 512, (tb + 1) * 512)
                pj = PB[i]
                for k in range(8):
                    p.op('pe', lambda e, pj=pj, k=k, blk=blk: e.matmul(pj[0:8, :], lhsT=wf_b[:, k, :], rhs=uT[:, k, blk], start=(k == 0), stop=(k == 7)),
                         r=['wf_b', ('uT', tb)], w=[('pj', i)])
                ev, spv, csb, r1, r2, hi, mid, lo = EV[i][0:8, :], SP[i][0:8, :], CSB[i][0:8, :], R1[i][0:8, :], R2[i][0:8, :], HI[i][0:8, :], MID[i][0:8, :], LO[i][0:8, :]
                p.op('act', lambda e, pj=pj, ev=ev: e.activation(out=ev, in_=pj[0:8, :], func=AF.Exp, bias=nfb[0:8, :], scale=-1.0), r=[('pj', i), 'nfb'], w=[('ev', i)])
                p.op('act', lambda e, ev=ev, spv=spv: e.activation(out=spv, in_=ev, func=AF.Ln, bias=oneb[0:8, :]), r=[('ev', i), 'c1'], w=[('sp', i)])
                init = 0.0 if tb == 0 else carry[0:8, :]
                p.op('dve', lambda e, csb=csb, spv=spv, init=init: e.tensor_tensor_scan(out=csb, data0=ones8[0:8, :], data1=spv, initial=init, op0=ALU.mult, op1=ALU.add),
                     r=[('sp', i), 'ones8', 'carry'], w=[('csb', i)])
                p.op('dve', lambda e, csb=csb: e.tensor_copy(out=carry[0:8, :], in_=csb[:, 511:512]), r=[('csb', i)], w=['carry'])
                p.op('act', lambda e, hi=hi, csb=csb: e.activation(out=hi, in_=csb, func=AF.Copy, scale=-1.0), r=[('csb', i)], w=[('hi', i)])
                p.op('dve', lambda e, r1=r1, csb=csb, hi=hi: e.tensor_tensor(out=r1, in0=csb, in1=hi, op=ALU.add), r=[('csb', i), ('hi', i)], w=[('r1', i)])
                p.op('act', lambda e, mid=mid, r1=r1: e.activation(out=mid, in_=r1, func=AF.Copy, scale=-1.0), r=[('r1', i)], w=[('mid', i)])
                p.op('dve', lambda e, r2=r2, r1=r1, mid=mid: e.tensor_tensor(out=r2, in0=r1, in1=mid, op=ALU.add), r=[('r1', i), ('mid', i)], w=[('r2', i)])
                p.op('act', lambda e, lo=lo, r2=r2: e.activation(out=lo, in_=r2, func=AF.Copy, scale=-1.0), r=[('r2', i)], w=[('lo', i)])
                dma(crow[0, :, blk], hi, [('hi', i)], ['crow'])
                dma(crow[1, :, blk], mid, [('mid', i)], ['crow'])
                dma(crow[2, :, blk], lo, [('lo', i)], ['crow'])
                for c in range(4):
                    p.op('pe', lambda e, c=c, csb=csb: e.transpose(out=PB[2][:, c * 8:(c + 1) * 8], in_=csb[:, c * 128:(c + 1) * 128], identity=identf[0:8, 0:8]),
                         r=[('csb', i), 'cf'], w=['pt8'])
                p.op('dve', lambda e, tb=tb: e.tensor_copy(out=ncol[:, tb * 4:(tb + 1) * 4, :], in_=PB[2][:, 0:32].rearrange("p (c h) -> p c h", c=4)), r=['pt8'], w=['ncol'])
            p.barrier()

        if 'fox' in stages:
            ar.reset()
            wst = ar.f32([128, 8, 192])
            wh = ar.bf16([128, 8, 192])
            wzst = wst[:, :, 0:128]
            wzb = ar.bf16([128, 8, 128])
            K_aug = ar.bf16([128, T])
            V_aug = ar.bf16([128, NCH, 128])
            Q_aug = [ar.bf16([128, 512]) for _ in range(2)]
            PTt = [ar.bf16([128, 1024]) for _ in range(3)]
            OC = [ar.f32([128, 512]) for _ in range(2)]
            rl = ar.f32([128, 512])
            tmpo = ar.f32([128, 512])
            szf = [ar.bf16([128, 512]) for _ in range(2)]
            obf = [ar.bf16([128, 512]) for _ in range(2)]
            EZ = [ar.f32([128, 512]) for _ in range(2)]
            ST = [PP[0], PP[1], PP[2]]
            OB = [PB[6], PB[7]]
            p.op('pool', lambda e: e.memset(K_aug[64:128, :], 0.0), w=['K_aug'])
            p.op('pool', lambda e: e.memset(K_aug[64:67, :], 1.0), w=['K_aug'])
            p.op('pool', lambda e: e.memset(V_aug[:, :, 64:128], 1.0), w=['V_aug'])
            for qi in range(2):
                p.op('pool', lambda e, qi=qi: e.memset(Q_aug[qi][64:128, :], 0.0), w=[('qa', qi)])
                dma(Q_aug[qi][67:70, :], cneg, [], [('qa', qi)])
            st_i = 0
            pt_i = 0
            qb_i = 0

            def next_st():
                nonlocal st_i
                r_ = (ST[st_i % 3], ('st', st_i % 3))
                st_i += 1
                return r_
            for h in range(8):
                dma(wst, wfox[h].rearrange("(k p) c -> p k c", p=128), [], ['wst'])
                p.op('pool', lambda e: e.tensor_copy(out=wh, in_=wst), r=['wst'], w=['wh'])
                if h % 2 == 0:
                    dma(wzst, wzf[:, (h // 2) * 128:(h // 2 + 1) * 128].rearrange("(k p) c -> p k c", p=128), [], ['wst'])
                    p.op('pool', lambda e: e.tensor_copy(out=wzb, in_=wzst), r=['wst'], w=['wzb'])
                dma(K_aug[67:70, :], crow[:, h, :], ['crow'], ['K_aug'])
                for tb in range(NB):
                    blk = slice(tb * 512, (tb + 1) * 512)
                    pj, pk = next_st()
                    for k in range(8):
                        p.op('pe', lambda e, pj=pj, k=k, blk=blk: e.matmul(pj[0:64, 0:512], lhsT=wh[:, k, 64:128], rhs=uT[:, k, blk], start=(k == 0), stop=(k == 7)),
                             r=['wh', ('uT', tb)], w=[pk])
                    p.op('dve', lambda e, pj=pj, blk=blk: e.tensor_copy(out=K_aug[0:64, blk], in_=pj[0:64, 0:512]), r=[pk], w=['K_aug'])
                for cb8 in range(NCH // 8):
                    pj, pk = next_st()
                    for c in range(8):
                        tok = slice((cb8 * 8 + c) * 128, (cb8 * 8 + c + 1) * 128)
                        for k in range(8):
                            p.op('pe', lambda e, pj=pj, c=c, k=k, tok=tok: e.matmul(pj[:, (c // 4) * 512 + (c % 4) * 64:(c // 4) * 512 + (c % 4) * 64 + 64], lhsT=uT[:, k, tok], rhs=wh[:, k, 128:192],
                                                                                    start=(k == 0), stop=(k == 7)),
                                 r=['wh', ('uT', cb8 * 2 + c // 4)], w=[pk])
                    for hb in range(2):
                        p.op('dve', lambda e, pj=pj, cb8=cb8, hb=hb: e.tensor_copy(out=V_aug[:, cb8 * 8 + hb * 4:cb8 * 8 + hb * 4 + 4, 0:64],
                                                                                 in_=pj[:, hb * 512:hb * 512 + 256].rearrange("p (c d) -> p c d", c=4)),
                             r=[pk], w=['V_aug'])
                half = slice((h % 2) * 64, (h % 2) * 64 + 64)

                def prologue(qb, qi, h=h):
                    blk = slice(qb * 512, (qb + 1) * 512)
                    qa = Q_aug[qi]
                    pj, pk = next_st()
                    for k in range(8):
                        p.op('pe', lambda e, pj=pj, k=k, blk=blk: e.matmul(pj[0:64, 0:512], lhsT=wh[:, k, 0:64], rhs=uT[:, k, blk], start=(k == 0), stop=(k == 7)),
                             r=['wh', ('uT', qb)], w=[pk])
                    for k in range(8):
                        p.op('pe', lambda e, pj=pj, k=k, blk=blk: e.matmul(pj[:, 512:1024], lhsT=wzb[:, k, :], rhs=uT[:, k, blk], start=(k == 0), stop=(k == 7)),
                             r=['wzb', ('uT', qb)], w=[pk])
                    p.op('dve', lambda e, pj=pj, qa=qa: e.tensor_scalar(out=qa[0:64, :], in0=pj[0:64, 0:512], scalar1=0.125, scalar2=None, op0=ALU.mult), r=[pk], w=[('qa', qi)])
                    dma(qa[64:67, :], crow[:, h, blk], ['crow'], [('qa', qi)])
                    ez = EZ[qi]
                    p.op('act', lambda e, pj=pj, ez=ez: e.activation(out=ez, in_=pj[:, 512:1024], func=AF.Exp, scale=-1.0), r=[pk], w=[('ez', qi)])
                    p.op('dve', lambda e, ez=ez: e.tensor_scalar(out=ez, in0=ez, scalar1=1.0, scalar2=None, op0=ALU.add), r=[('ez', qi)], w=[('ez', qi)])
                    p.op('dve', lambda e, ez=ez: e.reciprocal(out=ez, in_=ez), r=[('ez', qi)], w=[('ez', qi)])
                    sz = szf[qi]
                    p.op('dve', lambda e, pj=pj, ez=ez, sz=sz: e.tensor_tensor(out=sz, in0=pj[:, 512:1024], in1=ez, op=ALU.mult), r=[pk, ('ez', qi)], w=[('sz', qi)])

                def finalize2(qb, qi, h=h, half=half):
                    blk = slice(qb * 512, (qb + 1) * 512)
                    oc = OC[qi]
                    ob_ps = OB[qi]
                    ok = ('ob', qi)
                    sz = szf[qi]
                    of = obf[qi]
                    p.op('pe', lambda e: e.matmul(ob_ps, lhsT=swapm, rhs=oc, start=True, stop=True), r=['cf', ('oc', qi)], w=[ok])
                    if h % 2 == 0:
                        p.op('dve', lambda e: e.tensor_tensor(out=tmpo[0:64, :], in0=oc[0:64, :], in1=ob_ps[0:64, :], op=ALU.mult), r=[('oc', qi), ok], w=['tmpo'])
                        p.op('pool', lambda e: e.tensor_tensor(out=of[0:64, :], in0=tmpo[0:64, :], in1=sz[0:64, :], op=ALU.mult), r=['tmpo', ('sz', qi)], w=[('of', qi)])
                    else:
                        p.op('dve', lambda e: e.tensor_tensor(out=tmpo[64:128, :], in0=ob_ps[64:128, :], in1=rl[64:128, :], op=ALU.mult), r=['rl', ok], w=['tmpo'])
                        p.op('pool', lambda e: e.tensor_tensor(out=of[64:128, :], in0=tmpo[64:128, :], in1=sz[64:128, :], op=ALU.mult), r=['tmpo', ('sz', qi)], w=[('of', qi)])
                    r0 = 512 + h * 64
                    dma(mix_dst(r0, 64, blk), of[half, :], [('of', qi)], [('mo', len(p.ops))])
                    mo_keys[r0 // 128].append(('mo', len(p.ops) - 1))

                prologue(0, qb_i % 2)
                deferred = None
                for qb in range(NB):
                    qi = qb_i % 2
                    qb_i += 1
                    qa = Q_aug[qi]
                    nk = 4 * qb + 4
                    ob_ps = OB[qi]
                    ok = ('ob', qi)
                    pend = None
                    for kp in range(nk // 2):
                        st, sk = next_st()
                        for hh in range(2):
                            kc = 2 * kp + hh
                            diag = kc >= 4 * qb
                            ks = slice(kc * 128, (kc + 1) * 128)
                            dst = slice(hh * 512, hh * 512 + 512)
                            p.op('pe', lambda e, st=st, ks=ks, qa=qa, diag=diag, dst=dst: e.matmul(st[:, dst], lhsT=K_aug[:, ks], rhs=qa[:, :], start=True, stop=(not diag)),
                                 r=['K_aug', ('qa', qi)], w=[sk])
                            if diag:
                                j = kc - 4 * qb
                                p.op('pe', lambda e, st=st, j=j, dst=dst: e.matmul(st[:, dst], lhsT=identb, rhs=maskJ[:, j, :], start=False, stop=True), r=['cb'], w=[sk])
                        if pend is not None:
                            pend()
                        pt = PTt[pt_i % 3]
                        ptk = ('pt', pt_i % 3)
                        pt_i += 1
                        p.op('act', lambda e, st=st, pt=pt: e.activation(out=pt, in_=st[:, :], func=AF.Exp), r=[sk], w=[ptk])

                        def mk(pt=pt, ptk=ptk, kp=kp, nk=nk, ob_ps=ob_ps, ok=ok):
                            for hh in range(2):
                                kc = 2 * kp + hh
                                p.op('pe', lambda e, kc=kc, hh=hh: e.matmul(ob_ps, lhsT=V_aug[:, kc, :], rhs=pt[:, hh * 512:hh * 512 + 512], start=(kc == 0), stop=(kc == nk - 1)),
                                     r=['V_aug', ptk], w=[ok])
                        pend = mk
                        if kp == min(3, nk // 2 - 1):
                            if deferred is not None:
                                deferred()
                                deferred = None
                            if qb + 1 < NB:
                                prologue(qb + 1, qb_i % 2)
                    pend()
                    oc = OC[qi]
                    p.op('dve', lambda e, oc=oc, ob_ps=ob_ps: e.tensor_copy(out=oc, in_=ob_ps), r=[ok], w=[('oc', qi)])
                    if h % 2 == 0:
                        p.op('dve', lambda e, oc=oc: e.reciprocal(out=oc[64:128, :], in_=oc[64:128, :]), r=[('oc', qi)], w=[('oc', qi)])
                    else:
                        p.op('dve', lambda e, oc=oc: e.reciprocal(out=rl[64:128, :], in_=oc[64:128, :]), r=[('oc', qi)], w=['rl'])
                    deferred = (lambda qb=qb, qi=qi: finalize2(qb, qi))
                deferred()
        if fused:
            p.op('act', lambda e: e.activation(out=sm[:, 130:131], in_=sm[:, 25:26], func=AF.Copy), r=['c1'], w=[('fin', 'act')])
            p.op('dve', lambda e: e.tensor_copy(out=sm[:, 131:132], in_=sm[:, 25:26]), r=['c1'], w=[('fin', 'dve')])
            p.op('pool', lambda e: e.memset(sm[:, 132:133], 0.0), w=[('fin', 'pool')])
            for k in range(8):
                ml, Gk = fused['mixloc'][k], fused['G'][k]
                p.op('pool', lambda e, ml=ml, Gk=Gk: e.collective_compute("AllGather", ALU.bypass, replica_groups=[[0, 1], [2, 3], [4, 5], [6, 7]],
                                                                          ins=[ml.ap().opt()], outs=[Gk.ap().opt()]),
                     r=mo_keys[k] + [('fin', 'act'), ('fin', 'dve'), ('fin', 'pool')], w=[('G', k)], dma=True, cc=True)
            p.run(es, sems=fused['semsA'])
            fused['finalA'] = p.final_values()
        else:
            p.run(es)
    return nc


def build_B(nc, TO, fused=None):
    TB = TO + 128
    D = lambda name, shape, dt, kind=None: (nc.dram_tensor(name, shape, dt, kind=kind).ap() if kind else nc.dram_tensor(name, shape, dt).ap())
    if fused:
        Gbf = fused["G_bf"]
        halfS = fused["half"]
        xT = D("xTB", [1024, TB], F32, "ExternalInput")
        mseld = D("msel", [128, 2], F32, "ExternalInput")
    else:
        mixT = D("mixT", [2048, TB], BF16, "ExternalInput")
        xT = D("xT", [1024, TB], F32, "ExternalInput")
    if not fused:
        w_out0 = D("w_out0", [2048, 1024], F32, "ExternalInput")
        w_in1 = D("w_in1", [1024, 6144], F32, "ExternalInput")
        w_out1 = D("w_out1", [2048, 1024], F32, "ExternalInput")
    vecs = D("vecs", [128, 88], F32, "ExternalInput")
    cw1 = D("cw1", [128, 16, 31], F32, "ExternalInput")
    yT = D("yT", [1024, TO], F32, "ExternalOutput")
    if fused:
        s_out0, s_in1, s_out1 = fused['s_out0'], fused['s_in1'], fused['s_out1']
    else:
        s_out0 = D("s_out0", [8, 128, 16, 128], BF16)
        s_in1 = D("s_in1", [48, 128, 8, 128], BF16)
        s_out1 = D("s_out1", [8, 128, 16, 128], BF16)
    s_dg = fused["s_dg"] if fused else D("s_dg", [16, 128, 31 * 128], BF16)
    cbdB = D("cbdB", [128, 128], BF16, "ExternalInput")

    with ExitStack() as es:
        sb = lambda n, s, d: es.enter_context(nc.sbuf_tensor("b_" + n, s, d))
        ps = lambda n, s, d: es.enter_context(nc.psum_tensor("b_" + n, s, d))
        vec = sb("vec", [128, 88], F32)
        msel = sb("msel", [128, 2], F32)
        gpost0 = vec[:, 0:8]
        gpre1 = vec[:, 8:16]
        gpost1 = vec[:, 16:24]
        cb1 = vec[:, 24:40]
        lng = vec[:, 40:56]
        lnb = vec[:, 56:72]
        epsb = vec[:, 72:73]
        cw = sb("cw", [128, 16, 31], F32)
        onesb = sb("onesb", [128, 256], BF16)
        dgb = [sb("dgb%d" % i, [128, 31 * 128], BF16) for i in range(2)]
        wst = [dgb[i][:, 0:2048].bitcast(F32) for i in range(2)]
        wcb = [dgb[i][:, 2048:3072] for i in range(2)]
        identb = sb("identb", [128, 128], BF16)
        mx = sb("mx", [128, 16, 512], BF16)
        xb = sb("xb", [128, 8, 512], F32)
        o0 = sb("o0", [128, 8, 512], F32)
        sq = sb("sq", [128, 8, 512], BF16)
        tmp = sb("tmp", [128, 512], F32)
        rs = sb("rs", [128, 512], F32)
        u1 = sb("u1", [128, 8, 512], BF16)
        wo = [sb("wo%d" % i, [128, 16, 128], BF16) for i in range(3)]
        wi = [sb("wi%d" % i, [128, 8, 128], BF16) for i in range(4)]
        sig = [sb("sig%d" % i, [128, 512], F32) for i in range(2)]
        hbuf = sb("hbuf", [128, 16, 542], BF16)
        cbuf = sb("cbuf", [128, 16, 512], F32)
        c16 = [sb("c16_%d" % i, [128, 512], BF16) for i in range(2)]
        csq = [sb("csq%d" % i, [128, 512], BF16) for i in range(2)]
        mean = sb("mean", [128, 512], F32)
        var = sb("var", [128, 512], F32)
        rln = sb("rln", [128, 512], F32)
        t1 = [sb("t1_%d" % i, [128, 512], F32) for i in range(2)]
        s1 = [sb("s1_%d" % i, [128, 512], F32) for i in range(2)]
        szb = [sb("szb%d" % i, [128, 512], F32) for i in range(2)]
        h2 = sb("h2", [128, 16, 512], BF16)
        yo = [sb("yo%d" % i, [128, 512], F32) for i in range(2)]
        PB = [ps("pb%d" % i, [128, 512], F32) for i in range(8)]

        p = Prog(nc)
        dma = lambda out, in_, r, w: p.op('sp', lambda e: e.dma_start(out=out, in_=in_), r=r, w=w, dma=True)
        dma(vec[:], vecs, [], ['vec'])
        dma(cw[:], cw1, [], ['cw'])
        if fused:
            dma(msel[:], mseld, [], ['msel'])
        p.op('pool', lambda e: e.memset(epsb, EPS), r=['vec'], w=['vec2'])
        p.op('pool', lambda e: e.memset(onesb[:, 0:128], 1.0 / 1024), w=['c1'])
        p.op('pool', lambda e: e.memset(onesb[:, 128:256], 1.0 / 2048), w=['c1'])
        p.op('pool', lambda e: e.memset(hbuf[:, :, 0:30], 0.0), w=[('hb', j) for j in range(16)])

        if not fused:
            wi_ = 0
            pieces = []
            for ec in range(16):
                pieces.append((w_out0[ec * 128:(ec + 1) * 128, :], s_out0[:, :, ec, :].rearrange("d p c -> p d c"), 's_out0'))
            for kc in range(8):
                for cc in range(6):
                    pieces.append((w_in1[kc * 128:(kc + 1) * 128, cc * 1024:(cc + 1) * 1024], s_in1[cc * 8:(cc + 1) * 8, :, kc, :].rearrange("d p c -> p d c"), 's_in1'))
            for ec in range(16):
                pieces.append((w_out1[ec * 128:(ec + 1) * 128, :], s_out1[:, :, ec, :].rearrange("d p c -> p d c"), 's_out1'))
            for n, (src, dst, key) in enumerate(pieces):
                i = n % 2
                dma(wst[i], src, [], [('wst', i)])
                eng = 'pool' if n % 3 else 'dve'
                p.op(eng, lambda e, i=i: e.tensor_copy(out=wcb[i], in_=wst[i]), r=[('wst', i)], w=[('wcb', i)])
                dma(dst, wcb[i].rearrange("p (d c) -> p d c", d=8), [('wcb', i)], [key])
        p.barrier()
        dma(identb[:], cbdB, [], ['identb'])
        for j in range(0 if fused else 16):
            i = j % 2
            for s_ in range(31):
                eng = 'pool' if s_ % 2 else 'dve'
                p.op(eng, lambda e, i=i, j=j, s_=s_: e.tensor_scalar(out=dgb[i][:, s_ * 128:(s_ + 1) * 128], in0=identb[:], scalar1=cw[:, j, s_:s_ + 1], scalar2=None, op0=ALU.mult),
                     r=['identb', 'cw'], w=[('dgw', i, s_)])
            dma(s_dg[j], dgb[i][:], [('dgw', i, s_) for s_ in range(31)], ['s_dg'])
        p.barrier()

        cnt = dict(pj=0, wo=0, wi=0, aux=0)
        PJ = PB[0:3]
        PCV = [PB[3], PB[7]]
        PST = PB[4]
        PMEAN = PB[5]
        PMSQ = PB[6]

        def proj_out(src, skey, sname, W, dst_fn):
            for dj in range(8):
                wt = wo[cnt['wo'] % 3]
                wk = ('wo', cnt['wo'] % 3)
                cnt['wo'] += 1
                sc = s_out0 if sname == 's_out0' else s_out1
                dma(wt[:], sc[dj], [sname], [wk])
                pj = PJ[cnt['pj'] % 3]
                pk = ('pj', cnt['pj'] % 3)
                cnt['pj'] += 1
                for k in range(16):
                    p.op('pe', lambda e, pj=pj, wt=wt, k=k: e.matmul(pj[:, 0:W], lhsT=wt[:, k, :], rhs=src[:, k, 0:W], start=(k == 0), stop=(k == 15)),
                         r=[wk] + skey, w=[pk])
                dst_fn(dj, pj, pk)

        def rms_stats(W, srckey):
            for k in range(8):
                p.op('pe', lambda e, k=k: e.matmul(PST[:, 0:W], lhsT=onesb[:, 0:128], rhs=sq[:, k, 0:W], start=(k == 0), stop=(k == 7)), r=[('sq', k), 'c1'], w=['pst'])
            p.op('act', lambda e: e.activation(out=tmp[:, 0:W], in_=PST[:, 0:W], func=AF.Sqrt, bias=epsb), r=['pst', 'vec2'], w=['tmp'])
            p.op('dve', lambda e: e.reciprocal(out=rs[:, 0:W], in_=tmp[:, 0:W]), r=['tmp'], w=['rs'])

        blocks = [(0, 128)] + [(128 + 512 * i, 512) for i in range(TO // 512)]
        def load_mix(bi, s0, W):
            halo = (bi == 0)
            MXA = [('mx', k_) for k_ in range(8)]
            H2A = [('h2', k_) for k_ in range(8)]
            tb0 = halfS + s0 - 128
            for k in range(8):
                dma(h2[:, 2 * k:2 * k + 2, 0:W], Gbf[k][:, tb0:tb0 + W].rearrange("(r p) t -> p r t", p=128), [('G', k)], [('h2', k)])
            if halo:
                p.op('dve', lambda e: e.tensor_scalar(out=mx[:, :, 0:W], in0=h2[:, :, 0:W], scalar1=msel[:, 1:2], scalar2=None, op0=ALU.mult), r=H2A + ['msel'], w=MXA)
            else:
                ta0 = s0 - 128
                for k in range(8):
                    dma(mx[:, 2 * k:2 * k + 2, 0:W], Gbf[k][:, ta0:ta0 + W].rearrange("(r p) t -> p r t", p=128), [('G', k)], [('mx', k)])
                p.op('dve', lambda e: e.tensor_scalar(out=mx[:, :, 0:W], in0=mx[:, :, 0:W], scalar1=msel[:, 0:1], scalar2=None, op0=ALU.mult), r=MXA + ['msel'], w=MXA)
                p.op('dve', lambda e: e.scalar_tensor_tensor(out=mx[:, :, 0:W], in0=h2[:, :, 0:W], scalar=msel[:, 1:2], in1=mx[:, :, 0:W], op0=ALU.mult, op1=ALU.add),
                     r=H2A + MXA + ['msel'], w=MXA)

        def do_block(bi, s0, W):
            halo = (bi == 0)
            if not fused:
                dma(mx[:, :, 0:W], mixT[:, s0:s0 + W].rearrange("(k p) t -> p k t", p=128), [], [('mx', k_) for k_ in range(8)])
            elif bi == 0:
                load_mix(bi, s0, W)
            dma(xb[:, :, 0:W], xT[:, s0:s0 + W].rearrange("(k p) t -> p k t", p=128), [], [('xb', k) for k in range(8)])

            def ev0(dj, pj, pk):
                p.op('act', lambda e: e.activation(out=o0[:, dj, 0:W], in_=pj[:, 0:W], func=AF.Copy), r=[pk], w=[('o0', dj)])
                p.op('pool', lambda e: e.tensor_tensor(out=sq[:, dj, 0:W], in0=o0[:, dj, 0:W], in1=o0[:, dj, 0:W], op=ALU.mult), r=[('o0', dj)], w=[('sq', dj)])
            proj_out(mx, [('mx', k_) for k_ in range(8)], 's_out0', W, ev0)
            rms_stats(W, None)
            for k in range(8):
                p.op('dve', lambda e, k=k: e.scalar_tensor_tensor(out=o0[:, k, 0:W], in0=o0[:, k, 0:W], scalar=gpost0[:, k:k + 1], in1=rs[:, 0:W], op0=ALU.mult, op1=ALU.mult),
                     r=[('o0', k), 'rs', 'vec'], w=[('o0', k)])
                p.op('dve', lambda e, k=k: e.tensor_tensor(out=xb[:, k, 0:W], in0=xb[:, k, 0:W], in1=o0[:, k, 0:W], op=ALU.add), r=[('o0', k), ('xb', k)], w=[('xb', k)])
                p.op('pool', lambda e, k=k: e.tensor_tensor(out=sq[:, k, 0:W], in0=xb[:, k, 0:W], in1=xb[:, k, 0:W], op=ALU.mult), r=[('xb', k)], w=[('sq', k)])
            rms_stats(W, None)
            for k in range(8):
                p.op('dve', lambda e, k=k: e.scalar_tensor_tensor(out=u1[:, k, 0:W], in0=xb[:, k, 0:W], scalar=gpre1[:, k:k + 1], in1=rs[:, 0:W], op0=ALU.mult, op1=ALU.mult),
                     r=[('xb', k), 'rs', 'vec'], w=['u1'])
            if fused and bi + 1 < len(blocks):
                load_mix(bi + 1, blocks[bi + 1][0], blocks[bi + 1][1])

            def inproj(col):
                wt = wi[cnt['wi'] % 4]
                wk = ('wi', cnt['wi'] % 4)
                cnt['wi'] += 1
                dma(wt[:], s_in1[col // 128], ['s_in1'], [wk])
                pj = PJ[cnt['pj'] % 3]
                pk = ('pj', cnt['pj'] % 3)
                cnt['pj'] += 1
                for k in range(8):
                    p.op('pe', lambda e, pj=pj, wt=wt, k=k: e.matmul(pj[:, 0:W], lhsT=wt[:, k, :], rhs=u1[:, k, 0:W], start=(k == 0), stop=(k == 7)), r=[wk, 'u1'], w=[pk])
                return pj, pk

            for j in range(16):
                pv, pvk = inproj(j * 128)
                pg, pgk = inproj(2048 + j * 128)
                sg = sig[j % 2]
                p.op('act', lambda e, pg=pg, sg=sg: e.activation(out=sg[:, 0:W], in_=pg[:, 0:W], func=AF.Sigmoid), r=[pgk], w=[('sig', j % 2)])
                p.op('dve', lambda e, pv=pv, sg=sg, j=j: e.tensor_tensor(out=hbuf[:, j, 30:30 + W], in0=pv[:, 0:W], in1=sg[:, 0:W], op=ALU.mult), r=[pvk, ('sig', j % 2)], w=[('hb', j)])
            if halo:
                for j in range(16):
                    eng = 'dve' if j % 2 == 0 else 'pool'
                    p.op(eng, lambda e, j=j: e.tensor_copy(out=hbuf[:, j, 0:30], in_=hbuf[:, j, W:W + 30]), r=[('hb', j)], w=[('hb', j)])
                return
            for j in range(16):
                acc = cbuf[:, j, 0:W]
                di = cnt['aux'] % 2
                cnt['aux'] += 1
                dg = dgb[di]
                pcv = PCV[di]
                dma(dg[:], s_dg[j], ['s_dg'], [('dg', di)])
                for s in range(31):
                    p.op('pe', lambda e, pcv=pcv, dg=dg, j=j, s=s: e.matmul(pcv[:, 0:W], lhsT=dg[:, s * 128:(s + 1) * 128], rhs=hbuf[:, j, s:s + W], start=(s == 0), stop=(s == 30)),
                         r=[('dg', di), ('hb', j)], w=[('pcv', di)])
                p.op('act', lambda e, acc=acc, pcv=pcv, j=j: e.activation(out=acc, in_=pcv[:, 0:W], func=AF.Identity, bias=cb1[:, j:j + 1]), r=[('pcv', di), 'vec'], w=[('cb', j)])
                p.op('pool', lambda e, j=j: e.tensor_copy(out=hbuf[:, j, 0:30], in_=hbuf[:, j, W:W + 30]), r=[('hb', j)], w=[('hb', j)])
                a = j % 2
                p.op('act', lambda e, acc=acc, a=a: e.activation(out=c16[a][:, 0:W], in_=acc, func=AF.Copy), r=[('cb', j)], w=[('c16', a)])
                p.op('act', lambda e, acc=acc, a=a: e.activation(out=csq[a][:, 0:W], in_=acc, func=AF.Square), r=[('cb', j)], w=[('csq', a)])
                p.op('pe', lambda e, a=a, j=j: e.matmul(PMEAN[:, 0:W], lhsT=onesb[:, 128:256], rhs=c16[a][:, 0:W], start=(j == 0), stop=(j == 15)), r=[('c16', a), 'c1'], w=['pmean'])
                p.op('pe', lambda e, a=a, j=j: e.matmul(PMSQ[:, 0:W], lhsT=onesb[:, 128:256], rhs=csq[a][:, 0:W], start=(j == 0), stop=(j == 15)), r=[('csq', a), 'c1'], w=['pmsq'])
            p.op('act', lambda e: e.activation(out=mean[:, 0:W], in_=PMEAN[:, 0:W], func=AF.Copy), r=['pmean'], w=['mean'])
            p.op('dve', lambda e: e.tensor_tensor(out=var[:, 0:W], in0=mean[:, 0:W], in1=mean[:, 0:W], op=ALU.mult), r=['mean'], w=['var'])
            p.op('dve', lambda e: e.tensor_tensor(out=var[:, 0:W], in0=PMSQ[:, 0:W], in1=var[:, 0:W], op=ALU.subtract), r=['pmsq', 'var'], w=['var'])
            p.op('act', lambda e: e.activation(out=tmp[:, 0:W], in_=var[:, 0:W], func=AF.Sqrt, bias=epsb), r=['var', 'vec2'], w=['tmp'])
            p.op('dve', lambda e: e.reciprocal(out=rln[:, 0:W], in_=tmp[:, 0:W]), r=['tmp'], w=['rln'])
            for j in range(16):
                a = j % 2
                pz, pzk = inproj(4096 + j * 128)
                p.op('act', lambda e, pz=pz, a=a: e.activation(out=szb[a][:, 0:W], in_=pz[:, 0:W], func=AF.Silu), r=[pzk], w=[('szb', a)])
                p.op('dve', lambda e, j=j, a=a: e.tensor_tensor(out=t1[a][:, 0:W], in0=cbuf[:, j, 0:W], in1=mean[:, 0:W], op=ALU.subtract), r=[('cb', j), 'mean'], w=[('t1', a)])
                p.op('dve', lambda e, a=a: e.tensor_tensor(out=t1[a][:, 0:W], in0=t1[a][:, 0:W], in1=rln[:, 0:W], op=ALU.mult), r=[('t1', a), 'rln'], w=[('t1', a)])
                p.op('act', lambda e, j=j, a=a: e.activation(out=s1[a][:, 0:W], in_=t1[a][:, 0:W], func=AF.Silu, bias=lnb[:, j:j + 1], scale=lng[:, j:j + 1]), r=[('t1', a), 'vec'], w=[('s1', a)])
                p.op('pool', lambda e, j=j, a=a: e.tensor_tensor(out=h2[:, j, 0:W], in0=s1[a][:, 0:W], in1=szb[a][:, 0:W], op=ALU.mult), r=[('s1', a), ('szb', a)], w=[('h2', j // 2)])
            proj_out(h2, [('h2', k_) for k_ in range(8)], 's_out1', W, ev0)
            rms_stats(W, None)
            for k in range(8):
                y = yo[k % 2]
                p.op('dve', lambda e, k=k, y=y: e.scalar_tensor_tensor(out=y[:, 0:W], in0=o0[:, k, 0:W], scalar=gpost1[:, k:k + 1], in1=rs[:, 0:W], op0=ALU.mult, op1=ALU.mult),
                     r=[('o0', k), 'rs', 'vec'], w=[('yo', k % 2)])
                p.op('dve', lambda e, k=k, y=y: e.tensor_tensor(out=y[:, 0:W], in0=y[:, 0:W], in1=xb[:, k, 0:W], op=ALU.add), r=[('yo', k % 2), ('xb', k)], w=[('yo', k % 2)])
                dma(yT[k * 128:(k + 1) * 128, s0 - 128:s0 - 128 + W], y[:, 0:W], [('yo', k % 2)], [])
        for bi_, (s0_, W_) in enumerate(blocks):
            do_block(bi_, s0_, W_)
        if fused:
            p.run(es, sems=fused['semsB'], pre_waits=fused['finalA'])
        else:
            p.run(es)
    return nc


def build_fused(nc, S):
    half = S // 2
    mixloc = [nc.dram_tensor("mixloc%d" % k, [128, S // 2], F32) for k in range(8)]
    G = [nc.dram_tensor("Gmix%d" % k, [256, S // 2], F32) for k in range(8)]
    with ExitStack() as es0:
        fused = dict(mixloc=mixloc, G=G, mixloc_bf=[m.ap().bitcast(BF16) for m in mixloc], G_bf=[g.ap().bitcast(BF16) for g in G], half=half)
        w_out0 = nc.dram_tensor("w_out0", [2048, 1024], F32, kind="ExternalInput").ap()
        w_in1 = nc.dram_tensor("w_in1", [1024, 6144], F32, kind="ExternalInput").ap()
        w_out1 = nc.dram_tensor("w_out1", [2048, 1024], F32, kind="ExternalInput").ap()
        s_out0 = nc.dram_tensor("s_out0", [8, 128, 16, 128], BF16).ap()
        s_in1 = nc.dram_tensor("s_in1", [48, 128, 8, 128], BF16).ap()
        s_out1 = nc.dram_tensor("s_out1", [8, 128, 16, 128], BF16).ap()
        fused.update(s_out0=s_out0, s_in1=s_in1, s_out1=s_out1)
        pieces = []
        for ec in range(16):
            pieces.append((w_out0[ec * 128:(ec + 1) * 128, :].rearrange("p (d c) -> p d c", d=8), s_out0[:, :, ec, :].rearrange("d p c -> p d c")))
        for kc in range(8):
            for cc in range(6):
                pieces.append((w_in1[kc * 128:(kc + 1) * 128, cc * 1024:(cc + 1) * 1024].rearrange("p (d c) -> p d c", d=8),
                               s_in1[cc * 8:(cc + 1) * 8, :, kc, :].rearrange("d p c -> p d c")))
        for ec in range(16):
            pieces.append((w_out1[ec * 128:(ec + 1) * 128, :].rearrange("p (d c) -> p d c", d=8), s_out1[:, :, ec, :].rearrange("d p c -> p d c")))
        dgf = nc.dram_tensor("dgf", [16, 128, 31 * 128], F32, kind="ExternalInput").ap()
        s_dg = nc.dram_tensor("s_dg", [16, 128, 31 * 128], BF16).ap()
        fused['s_dg'] = s_dg
        for j in range(16):
            pieces.append((dgf[j], s_dg[j]))
        fused['cast_pieces'] = pieces
        fused['semsA'] = Prog.make_sems(nc, es0, "a")
        fused['semsB'] = Prog.make_sems(nc, es0, "b")
        build_A(nc, S, fused=fused)
        build_B(nc, half, fused=fused)
    return nc


_PROGS = {}


def _get_prog(kind, T):
    key = (kind, T)
    if key not in _PROGS:
        nc = bass.Bass("TRN2", target_bir_lowering=False)
        if kind == 'A':
            build_A(nc, T)
        else:
            build_B(nc, T)
        _PROGS[key] = nc
    return _PROGS[key]


def _pk(v, n):
    return np.ascontiguousarray(np.asarray(v, np.float32).reshape(n, 128).T)


def _inputs_A(b, p, x, e_norm_pre, e_w_in, e_conv_w, e_conv_b, e_dt_bias, e_a_log, e_d_skip, e_fgate_b, e_ssd_norm, cf, cb):
    w = e_w_in[0]
    cwv = e_conv_w[0]
    cbv_ = e_conv_b[0]
    wssd = np.empty((2, 1024, 768), np.float32)
    cw = np.empty((128, 8, 4), np.float32)
    cbv = np.empty((128, 8), np.float32)
    dtb16 = np.empty((128, 2, 16), np.float32)
    alog16 = np.empty((128, 2, 16), np.float32)
    dsk = np.empty((128, 4), np.float32)
    ssdn = np.empty((128, 4), np.float32)
    pp = np.arange(128)
    for gl in range(2):
        G = 2 * p + gl
        wssd[gl, :, 0:256] = w[:, 256 * G:256 * G + 256]
        wssd[gl, :, 256:512] = w[:, 2048 + 256 * G:2048 + 256 * G + 256]
        wssd[gl, :, 512:640] = w[:, 3072 + 128 * G:3072 + 128 * G + 128]
        wssd[gl, :, 640:768] = w[:, 3584 + 128 * G:3584 + 128 * G + 128]
        chans = [256 * G + pp, 256 * G + 128 + pp, 1024 + 128 * G + pp, 1536 + 128 * G + pp]
        for j in range(4):
            cw[:, gl * 4 + j, :] = cwv[:, chans[j]].T
            cbv[:, gl * 4 + j] = cbv_[chans[j]]
        for c in range(4):
            dtb16[:, gl, c * 4:(c + 1) * 4] = e_dt_bias[0][4 * G:4 * G + 4][None, :]
            alog16[:, gl, c * 4:(c + 1) * 4] = e_a_log[0][4 * G:4 * G + 4][None, :]
        for pr in range(2):
            dsk[:, gl * 2 + pr] = e_d_skip[0][4 * G + (pr * 128 + pp) // 64]
            ssdn[:, gl * 2 + pr] = e_ssd_norm[0][256 * G + pr * 128 + pp]
    wfox = np.empty((8, 1024, 192), np.float32)
    for hl in range(8):
        H = 8 * p + hl
        wfox[hl, :, 0:64] = w[:, 4112 + 64 * H:4112 + 64 * H + 64]
        wfox[hl, :, 64:128] = w[:, 5136 + 64 * H:5136 + 64 * H + 64]
        wfox[hl, :, 128:192] = w[:, 6160 + 64 * H:6160 + 64 * H + 64]
    return dict(
        xT=np.ascontiguousarray(x[b].T), gpre=_pk(e_norm_pre[0], 8), wssd=wssd,
        wdt=np.ascontiguousarray(w[:, 4096 + 8 * p:4096 + 8 * p + 8]), cw=cw, cbv=cbv, dtb16=dtb16, alog16=alog16,
        dsk=dsk, ssdn=ssdn, wfox=wfox, wzf=np.ascontiguousarray(w[:, 1024 + 512 * p:1024 + 512 * p + 512]),
        wf=np.ascontiguousarray(w[:, 7184 + 8 * p:7184 + 8 * p + 8]),
        fb=np.ascontiguousarray(e_fgate_b[0][8 * p:8 * p + 8].reshape(8, 1)), cfd=cf, cbd=cb,
        cneg=np.full((3, 512), -1.0, ml_dtypes.bfloat16))


def kernel_unfused(x, e_norm_pre, e_w_in, e_conv_w, e_conv_b, e_dt_bias, e_a_log, e_d_skip, e_fgate_b,
           e_ssd_norm, e_w_out, e_norm_post, o_norm_pre, o_w_in, o_conv_w, o_conv_b, o_ln_g, o_ln_b,
           o_w_out, o_norm_post, _debug=None):
    args = [np.asarray(a, np.float32) for a in (x, e_norm_pre, e_w_in, e_conv_w, e_conv_b, e_dt_bias, e_a_log, e_d_skip, e_fgate_b,
                                                 e_ssd_norm, e_w_out, e_norm_post, o_norm_pre, o_w_in, o_conv_w, o_conv_b, o_ln_g, o_ln_b,
                                                 o_w_out, o_norm_post)]
    (x, e_norm_pre, e_w_in, e_conv_w, e_conv_b, e_dt_bias, e_a_log, e_d_skip, e_fgate_b,
     e_ssd_norm, e_w_out, e_norm_post, o_norm_pre, o_w_in, o_conv_w, o_conv_b, o_ln_g, o_ln_b, o_w_out, o_norm_post) = args
    Bn, S, Dm = x.shape
    half = S // 2
    cf, cb = _consts_np()
    ncA = _get_prog('A', S)
    mapsA = [_inputs_A(c // 2, c % 2, x, e_norm_pre, e_w_in, e_conv_w, e_conv_b, e_dt_bias, e_a_log, e_d_skip, e_fgate_b, e_ssd_norm, cf, cb)
             for c in range(8)]
    resA = run_bass_kernel_spmd(ncA, mapsA, core_ids=list(range(8)))
    mix = []
    for b in range(Bn):
        m0 = np.asarray(resA.results[2 * b]["mixT"])
        m1 = np.asarray(resA.results[2 * b + 1]["mixT"])
        mix.append(np.concatenate([m0[0:512], m1[0:512], m0[512:1024], m1[512:1024]], axis=0))
    if _debug is not None:
        _debug['mix'] = mix
    ncB = _get_prog('B', half)
    vecs = np.zeros((128, 88), np.float32)
    vecs[:, 0:8] = _pk(e_norm_post[0], 8)
    vecs[:, 8:16] = _pk(o_norm_pre[0], 8)
    vecs[:, 16:24] = _pk(o_norm_post[0], 8)
    vecs[:, 24:40] = _pk(o_conv_b[0], 16)
    vecs[:, 40:56] = _pk(o_ln_g[0], 16)
    vecs[:, 56:72] = _pk(o_ln_b[0], 16)
    cw1 = np.ascontiguousarray(o_conv_w[0].reshape(31, 16, 128).transpose(2, 1, 0))
    mapsB = []
    for c in range(8):
        b, p = c // 2, c % 2
        mT = np.zeros((2048, half + 128), ml_dtypes.bfloat16)
        xTt = np.zeros((1024, half + 128), np.float32)
        lo = half * p - 128
        if lo < 0:
            mT[:, 128:] = mix[b][:, 0:half]
            xTt[:, 128:] = x[b, 0:half].T
        else:
            mT[:] = mix[b][:, lo:lo + half + 128]
            xTt[:] = x[b, lo:lo + half + 128].T
        mapsB.append(dict(mixT=mT, xT=xTt, w_out0=e_w_out[0], w_in1=o_w_in[0], w_out1=o_w_out[0], vecs=vecs, cw1=cw1, cbdB=np.ascontiguousarray(cb[:, 0:128])))
    resB = run_bass_kernel_spmd(ncB, mapsB, core_ids=list(range(8)))
    out = np.empty((Bn, S, Dm), np.float32)
    for c in range(8):
        b, p = c // 2, c % 2
        out[b, half * p:half * (p + 1), :] = np.asarray(resB.results[c]["yT"]).T
    return out


def kernel(x, e_norm_pre, e_w_in, e_conv_w, e_conv_b, e_dt_bias, e_a_log, e_d_skip, e_fgate_b,
           e_ssd_norm, e_w_out, e_norm_post, o_norm_pre, o_w_in, o_conv_w, o_conv_b, o_ln_g, o_ln_b,
           o_w_out, o_norm_post):
    args = [np.asarray(a, np.float32) for a in (x, e_norm_pre, e_w_in, e_conv_w, e_conv_b, e_dt_bias, e_a_log, e_d_skip, e_fgate_b,
                                                 e_ssd_norm, e_w_out, e_norm_post, o_norm_pre, o_w_in, o_conv_w, o_conv_b, o_ln_g, o_ln_b,
                                                 o_w_out, o_norm_post)]
    (x, e_norm_pre, e_w_in, e_conv_w, e_conv_b, e_dt_bias, e_a_log, e_d_skip, e_fgate_b,
     e_ssd_norm, e_w_out, e_norm_post, o_norm_pre, o_w_in, o_conv_w, o_conv_b, o_ln_g, o_ln_b, o_w_out, o_norm_post) = args
    Bn, S, Dm = x.shape
    half = S // 2
    cf, cb = _consts_np()
    key = ('F', S)
    if key not in _PROGS:
        nc = bass.Bass("TRN2", target_bir_lowering=False)
        build_fused(nc, S)
        _PROGS[key] = nc
    nc = _PROGS[key]
    vecs = np.zeros((128, 88), np.float32)
    vecs[:, 0:8] = _pk(e_norm_post[0], 8)
    vecs[:, 8:16] = _pk(o_norm_pre[0], 8)
    vecs[:, 16:24] = _pk(o_norm_post[0], 8)
    vecs[:, 24:40] = _pk(o_conv_b[0], 16)
    vecs[:, 40:56] = _pk(o_ln_g[0], 16)
    vecs[:, 56:72] = _pk(o_ln_b[0], 16)
    cw1 = np.ascontiguousarray(o_conv_w[0].reshape(31, 16, 128).transpose(2, 1, 0))
    wo = e_w_out[0]
    rows = []
    for k in range(8):
        for r in range(2):
            base = (512 * r + 128 * k) if k < 4 else (1024 + 512 * r + 128 * (k - 4))
            rows.append(wo[base:base + 128])
    w_out0 = np.ascontiguousarray(np.concatenate(rows, axis=0))
    dgf = np.zeros((16, 128, 31, 128), np.float32)
    ii = np.arange(128)
    for j in range(16):
        dgf[j, ii, :, ii] = o_conv_w[0][:, j * 128:(j + 1) * 128].T
    dgf = dgf.reshape(16, 128, 31 * 128)
    maps = []
    for c in range(8):
        b, p = c // 2, c % 2
        m = _inputs_A(b, p, x, e_norm_pre, e_w_in, e_conv_w, e_conv_b, e_dt_bias, e_a_log, e_d_skip, e_fgate_b, e_ssd_norm, cf, cb)
        xTt = np.zeros((1024, half + 128), np.float32)
        lo = half * p - 128
        if lo < 0:
            xTt[:, 128:] = x[b, 0:half].T
        else:
            xTt[:] = x[b, lo:lo + half + 128].T
        msel = np.zeros((128, 2), np.float32)
        msel[:, p] = 1.0
        m.update(xTB=xTt, msel=msel, w_out0=w_out0, w_in1=o_w_in[0], w_out1=o_w_out[0], vecs=vecs, cw1=cw1, cbdB=np.ascontiguousarray(cb[:, 0:128]), dgf=dgf)
        maps.append(m)
    res = run_bass_kernel_spmd(nc, maps, core_ids=list(range(8)))
    out = np.empty((Bn, S, Dm), np.float32)
    for c in range(8):
        b, p = c // 2, c % 2
        out[b, half * p:half * (p + 1), :] = np.asarray(res.results[c]["yT"]).T
    return out
```
